# Optimizing a Trainium2 kernel written in Bass

```python
import jax
import jax.numpy as jnp
from jax import lax
import numpy as np

D_MODEL = 4096
BATCH = 4
SEQ = 2048
DEPTH = 2

D_FF = 4 * D_MODEL
NORM_EPS = 1e-6
LRU_WIDTH = D_MODEL // 2
LRU_HEADS = 8
LRU_BLOCK = LRU_WIDTH // LRU_HEADS
CONV_WIDTH = 4
LRU_C = 8.0
RWKV_WIDTH = D_MODEL // 2
RWKV_HEAD = 64
RWKV_HEADS = RWKV_WIDTH // RWKV_HEAD
DECAY_LORA = 96
AAA_LORA = 96
GATE_LORA = 256
RWKV_COLS = 3 * RWKV_WIDTH + DECAY_LORA + AAA_LORA + GATE_LORA
GN_EPS = 64e-5
HY_IN = 2 * LRU_WIDTH + RWKV_COLS
ML_HEADS = 8
ML_QK = D_MODEL // 2
ML_V = D_MODEL
ML_DQK = ML_QK // ML_HEADS
ML_DV = ML_V // ML_HEADS
ML_CHUNK = 64
ML_IN = 2 * ML_QK + 2 * ML_V + 2 * ML_HEADS
N_EVEN = (DEPTH + 1) // 2
N_ODD = DEPTH // 2

kernel_name = "hybrid_rglru_rwkv7_mlstm_block"

F32 = jnp.float32


def rmsnorm(x, g, eps=NORM_EPS):
    xf = x.astype(F32)
    y = xf * lax.rsqrt(jnp.mean(xf * xf, axis=-1, keepdims=True) + eps)
    return (y * g.astype(F32)).astype(x.dtype)


def shift_right(z):
    return jnp.pad(z, ((0, 0), (1, 0), (0, 0)))[:, :-1]


def causal_depthwise_conv(u, w, b):
    seq = u.shape[1]
    up = jnp.pad(u, ((0, 0), (CONV_WIDTH - 1, 0), (0, 0)))
    out = b
    for j in range(CONV_WIDTH):
        out = out + w[j] * up[:, j:j + seq]
    return out


def linear_recurrence_combine(c1, c2):
    a1, b1 = c1
    a2, b2 = c2
    return a1 * a2, a2 * b1 + b2


def rglru_branch(z, conv_w, conv_b, w_a, b_a, w_x, b_x, lam):
    bsz, seq, _ = z.shape
    u = z[..., :LRU_WIDTH].astype(F32)
    gate = jax.nn.gelu(z[..., LRU_WIDTH:].astype(F32))
    u = causal_depthwise_conv(u, conv_w.astype(F32), conv_b.astype(F32))
    ub = u.reshape(bsz, seq, LRU_HEADS, LRU_BLOCK)
    r = jax.nn.sigmoid(jnp.einsum("bshi,hij->bshj", ub, w_a.astype(F32)).reshape(bsz, seq, LRU_WIDTH) + b_a.astype(F32))
    i = jax.nn.sigmoid(jnp.einsum("bshi,hij->bshj", ub, w_x.astype(F32)).reshape(bsz, seq, LRU_WIDTH) + b_x.astype(F32))
    log_a = -LRU_C * r * jax.nn.softplus(-lam.astype(F32))
    a = jnp.exp(log_a)
    b = jnp.sqrt(-jnp.expm1(2.0 * log_a)) * (i * u)
    _, hseq = lax.associative_scan(linear_recurrence_combine, (a, b), axis=1)
    return hseq * gate


def rwkv7_branch(z, mu, w0, w2, a0, a2, g2, k_k, k_a, r_k, ln_w, ln_b):
    bsz, seq, _ = z.shape
    w0, w2, a0, a2, g2, k_k, k_a, r_k, ln_w, ln_b = (
        p.astype(F32) for p in (w0, w2, a0, a2, g2, k_k, k_a, r_k, ln_w, ln_b))
    z = z.astype(F32)
    z = z + (shift_right(z) - z) * mu.astype(F32)
    W = RWKV_WIDTH
    r = z[..., :W]
    k = z[..., W:2 * W]
    v = z[..., 2 * W:3 * W]
    o = 3 * W
    wl = z[..., o:o + DECAY_LORA]
    o += DECAY_LORA
    al = z[..., o:o + AAA_LORA]
    o += AAA_LORA
    gl = z[..., o:o + GATE_LORA]
    w = -jax.nn.softplus(-(w0 + jnp.tanh(wl) @ w2)) - 0.5
    decay = jnp.exp(-jnp.exp(w))
    a = jax.nn.sigmoid(a0 + al @ a2)
    g = jax.nn.sigmoid(gl) @ g2

    def hd(t):
        return t.reshape(bsz, seq, RWKV_HEADS, RWKV_HEAD)

    kk = hd(k * k_k)
    kk = kk / jnp.maximum(jnp.linalg.norm(kk, axis=-1, keepdims=True), 1e-12)
    k = k * (1.0 + (a - 1.0) * k_a)
    r, k, v, decay, a = (hd(t) for t in (r, k, v, decay, a))

    def tm(t):
        return jnp.swapaxes(t, 0, 1)

    def step(state, inp):
        r_t, w_t, k_t, v_t, kk_t, kka_t = inp
        sa = jnp.einsum("bhvk,bhk->bhv", state, kk_t)
        state = (state * w_t[:, :, None, :]
                 - sa[..., :, None] * kka_t[..., None, :]
                 + v_t[..., :, None] * k_t[..., None, :])
        return state, jnp.einsum("bhvk,bhk->bhv", state, r_t)

    s0 = jnp.zeros((bsz, RWKV_HEADS, RWKV_HEAD, RWKV_HEAD), F32)
    _, y = lax.scan(step, s0, (tm(r), tm(decay), tm(k), tm(v), tm(kk), tm(kk * a)))
    y = tm(y)
    mean = jnp.mean(y, axis=-1, keepdims=True)
    var = jnp.mean(jnp.square(y - mean), axis=-1, keepdims=True)
    y = (y - mean) * lax.rsqrt(var + GN_EPS) * ln_w.reshape(RWKV_HEADS, RWKV_HEAD) + ln_b.reshape(RWKV_HEADS, RWKV_HEAD)
    bonus = jnp.sum(r * k * r_k.reshape(RWKV_HEADS, RWKV_HEAD), axis=-1, keepdims=True) * v
    return (y + bonus).reshape(bsz, seq, W) * g


def mlstm_chunk_step(carry, inp):
    c_mat, n_vec, m_st = carry
    q, k, v, li, lf = inp
    L = q.shape[2]
    b = jnp.cumsum(lf, axis=-1)
    causal = jnp.tril(jnp.ones((L, L), dtype=bool))
    log_d = jnp.where(causal, b[..., :, None] - b[..., None, :] + li[..., None, :], -jnp.inf)
    log_inter = b + m_st[..., None]
    m_t = jnp.maximum(jnp.max(log_d, axis=-1), log_inter)
    p = jnp.einsum("bhld,bhsd->bhls", q, k) * jnp.exp(log_d - m_t[..., None])
    inter = jnp.exp(log_inter - m_t)
    num = jnp.einsum("bhls,bhsv->bhlv", p, v) + inter[..., None] * jnp.einsum("bhld,bhdv->bhlv", q, c_mat)
    den = jnp.sum(p, axis=-1) + inter * jnp.einsum("bhld,bhd->bhl", q, n_vec)
    h = num / jnp.maximum(jnp.abs(den), jnp.exp(-m_t))[..., None]
    b_last = b[..., -1]
    log_g = b_last[..., None] - b + li
    m_new = jnp.maximum(b_last + m_st, jnp.max(log_g, axis=-1))
    carry_decay = jnp.exp(b_last + m_st - m_new)
    wk = k * jnp.exp(log_g - m_new[..., None])[..., None]
    c_new = carry_decay[..., None, None] * c_mat + jnp.einsum("bhsd,bhsv->bhdv", wk, v)
    n_new = carry_decay[..., None] * n_vec + jnp.sum(wk, axis=2)
    return (c_new, n_new, m_new), h


def mlstm_mixer(hn, w_in, b_i, b_f, norm_w, w_out):
    bsz, seq, _ = hn.shape
    nc = seq // ML_CHUNK
    z = (hn @ w_in).astype(F32)
    o = 0
    zq = z[..., o:o + ML_QK]
    o += ML_QK
    zk = z[..., o:o + ML_QK]
    o += ML_QK
    zv = z[..., o:o + ML_V]
    o += ML_V
    zo = z[..., o:o + ML_V]
    o += ML_V
    zi = z[..., o:o + ML_HEADS]
    o += ML_HEADS
    zf = z[..., o:o + ML_HEADS]

    def chunk_heads(t, d):
        return t.reshape(bsz, nc, ML_CHUNK, ML_HEADS, d).transpose(1, 0, 3, 2, 4)

    def chunk_gates(t):
        return t.reshape(bsz, nc, ML_CHUNK, ML_HEADS).transpose(1, 0, 3, 2)

    q = chunk_heads(zq * (ML_DQK ** -0.5), ML_DQK)
    k = chunk_heads(zk, ML_DQK)
    v = chunk_heads(zv, ML_DV)
    li = chunk_gates(zi + b_i.astype(F32))
    lf = chunk_gates(jax.nn.log_sigmoid(zf + b_f.astype(F32)))
    init = (jnp.zeros((bsz, ML_HEADS, ML_DQK, ML_DV), F32),
            jnp.zeros((bsz, ML_HEADS, ML_DQK), F32),
            jnp.zeros((bsz, ML_HEADS), F32))
    _, hs = lax.scan(mlstm_chunk_step, init, (q, k, v, li, lf))
    hs = hs.transpose(1, 0, 3, 2, 4).reshape(bsz, seq, ML_HEADS, ML_DV)
    hs = hs * lax.rsqrt(jnp.mean(hs * hs, axis=-1, keepdims=True) + NORM_EPS)
    hs = hs.reshape(bsz, seq, ML_V) * norm_w.astype(F32) * jax.nn.sigmoid(zo)
    return hs.astype(hn.dtype) @ w_out


def sq_relu_mlp(h, w_up, w_down):
    u = jax.nn.relu(h @ w_up)
    return (u * u) @ w_down


def setup_inputs(seed: int = 0) -> dict:
    key = jax.random.key(seed)
    ks = iter(jax.random.split(key, 40))

    def nrm(shape, scale):
        return jax.random.normal(next(ks), shape, F32) * scale

    def unif(shape, lo, hi):
        return jax.random.uniform(next(ks), shape, F32, lo, hi)

    a_base = unif((N_EVEN, LRU_WIDTH), 0.9, 0.999)
    return {
        "x": nrm((BATCH, SEQ, D_MODEL), 1.0),
        "norm_mix": 1.0 + nrm((DEPTH, D_MODEL), 0.02),
        "norm_mlp": 1.0 + nrm((DEPTH, D_MODEL), 0.02),
        "norm_final": 1.0 + nrm((D_MODEL,), 0.02),
        "mlp_up": nrm((DEPTH, D_MODEL, D_FF), D_MODEL ** -0.5),
        "mlp_down": nrm((DEPTH, D_FF, D_MODEL), D_FF ** -0.5),
        "hy_in": nrm((N_EVEN, D_MODEL, HY_IN), D_MODEL ** -0.5),
        "lru_conv_w": nrm((N_EVEN, CONV_WIDTH, LRU_WIDTH), CONV_WIDTH ** -0.5),
        "lru_conv_b": nrm((N_EVEN, LRU_WIDTH), 0.02),
        "lru_wa": nrm((N_EVEN, LRU_HEADS, LRU_BLOCK, LRU_BLOCK), LRU_BLOCK ** -0.5),
        "lru_ba": nrm((N_EVEN, LRU_WIDTH), 0.1),
        "lru_wx": nrm((N_EVEN, LRU_HEADS, LRU_BLOCK, LRU_BLOCK), LRU_BLOCK ** -0.5),
        "lru_bx": nrm((N_EVEN, LRU_WIDTH), 0.1),
        "lru_lam": jnp.log(a_base) - jnp.log1p(-a_base),
        "rwkv_mu": unif((N_EVEN, RWKV_COLS), 0.0, 1.0),
        "rwkv_w0": unif((N_EVEN, RWKV_WIDTH), -6.0, -1.0),
        "rwkv_w2": nrm((N_EVEN, DECAY_LORA, RWKV_WIDTH), 0.5 * DECAY_LORA ** -0.5),
        "rwkv_a0": nrm((N_EVEN, RWKV_WIDTH), 0.1),
        "rwkv_a2": nrm((N_EVEN, AAA_LORA, RWKV_WIDTH), 0.5 * AAA_LORA ** -0.5),
        "rwkv_g2": nrm((N_EVEN, GATE_LORA, RWKV_WIDTH), GATE_LORA ** -0.5),
        "rwkv_kk": 0.85 + nrm((N_EVEN, RWKV_WIDTH), 0.05),
        "rwkv_ka": 1.0 + nrm((N_EVEN, RWKV_WIDTH), 0.05),
        "rwkv_rk": nrm((N_EVEN, RWKV_WIDTH), 0.1),
        "rwkv_ln_w": 1.0 + nrm((N_EVEN, RWKV_WIDTH), 0.02),
        "rwkv_ln_b": nrm((N_EVEN, RWKV_WIDTH), 0.02),
        "hy_out": nrm((N_EVEN, LRU_WIDTH + RWKV_WIDTH, D_MODEL), (LRU_WIDTH + RWKV_WIDTH) ** -0.5),
        "ml_in": nrm((N_ODD, D_MODEL, ML_IN), D_MODEL ** -0.5),
        "ml_bi": -2.0 + nrm((N_ODD, ML_HEADS), 0.1),
        "ml_bf": unif((N_ODD, ML_HEADS), 3.0, 6.0),
        "ml_norm": 1.0 + nrm((N_ODD, ML_V), 0.02),
        "ml_out": nrm((N_ODD, ML_V, D_MODEL), ML_V ** -0.5),
    }


def reference(x, norm_mix, norm_mlp, norm_final, mlp_up, mlp_down, hy_in, lru_conv_w, lru_conv_b,
              lru_wa, lru_ba, lru_wx, lru_bx, lru_lam, rwkv_mu, rwkv_w0, rwkv_w2, rwkv_a0, rwkv_a2,
              rwkv_g2, rwkv_kk, rwkv_ka, rwkv_rk, rwkv_ln_w, rwkv_ln_b, hy_out, ml_in, ml_bi, ml_bf,
              ml_norm, ml_out):
    for layer in range(DEPTH):
        h = rmsnorm(x, norm_mix[layer])
        if layer % 2 == 0:
            e = layer // 2
            z = h @ hy_in[e]
            ya = rglru_branch(z[..., :2 * LRU_WIDTH], lru_conv_w[e], lru_conv_b[e], lru_wa[e], lru_ba[e],
                              lru_wx[e], lru_bx[e], lru_lam[e])
            yb = rwkv7_branch(z[..., 2 * LRU_WIDTH:], rwkv_mu[e], rwkv_w0[e], rwkv_w2[e], rwkv_a0[e],
                              rwkv_a2[e], rwkv_g2[e], rwkv_kk[e], rwkv_ka[e], rwkv_rk[e],
                              rwkv_ln_w[e], rwkv_ln_b[e])
            mix = jnp.concatenate([ya, yb], axis=-1).astype(x.dtype) @ hy_out[e]
        else:
            o = layer // 2
            mix = mlstm_mixer(h, ml_in[o], ml_bi[o], ml_bf[o], ml_norm[o], ml_out[o])
        x = x + mix
        x = x + sq_relu_mlp(rmsnorm(x, norm_mlp[layer]), mlp_up[layer], mlp_down[layer])
    return rmsnorm(x, norm_final)
```

```python
import numpy as np
from contextlib import ExitStack
import concourse.bass as bass
import concourse.mybir as mybir
from concourse.bass_utils import run_bass_kernel_spmd
import ml_dtypes

F32 = mybir.dt.float32
BF16 = mybir.dt.bfloat16
AF = mybir.ActivationFunctionType
ALU = mybir.AluOpType
AX = mybir.AxisListType

D_MODEL = 4096
SEQ = 2048
BATCH = 4
D_FF = 16384
NORM_EPS = 1e-6


class _Op:
    __slots__ = ("eng", "fn", "deps", "needed", "sem", "val", "dma", "inc")

    def __init__(self, eng, fn, dma=None, inc=16):
        self.eng = eng
        self.fn = fn
        self.deps = ()
        self.needed = False
        self.sem = None
        self.val = 0
        self.dma = dma
        self.inc = inc


class Prog:
    ENGS = ("pe", "dve", "act", "pool", "sp")

    def __init__(self, nc, stack):
        self.nc = nc
        self.stack = stack
        self.streams = {e: [] for e in self.ENGS}
        self.lastw = {}
        self.readers = {}
        self.dma_sems = {}
        self.dma_last = {}
        self.nops = 0
        self.master = None
        self.banks = None
        self.bank_i = 0

    def sb(self, name, shape, dt):
        if self.master is not None:
            return self.master.f32(shape) if dt == F32 else self.master.bf16(shape)
        return self.stack.enter_context(self.nc.sbuf_tensor(name, list(shape), dt))

    def ps(self, name, shape, dt=F32):
        if self.banks is not None:
            bk = self.banks[self.bank_i]
            self.bank_i += 1
            if dt == F32:
                return bk
            return bk[:, :].bitcast(BF16)
        return self.stack.enter_context(self.nc.psum_tensor(name, list(shape), dt))

    def _record(self, o, r, w):
        deps = set()
        for k in r:
            d = self.lastw.get(k)
            if d is not None:
                deps.add(d)
        for k in w:
            d = self.lastw.get(k)
            if d is not None:
                deps.add(d)
            rd = self.readers.get(k)
            if rd:
                deps.update(rd.values())
        deps.discard(o)
        o.deps = tuple(deps)
        for d in deps:
            d.needed = True
        self.streams[o.eng].append(o)
        tag = o.dma if o.dma is not None else o.eng
        for k in r:
            self.readers.setdefault(k, {})[tag] = o
        for k in w:
            self.lastw[k] = o
            self.readers[k] = {}
        self.nops += 1
        return o

    def op(self, eng, fn, r=(), w=()):
        return self._record(_Op(eng, fn), r, w)

    def dma(self, queue, fn, r=(), w=(), sem=None, inc=16):
        assert sem is not None
        o = self._record(_Op(queue, fn, dma=sem, inc=inc), r, w)
        self.dma_last[sem] = o
        return o

    def barrier(self):
        lasts = []
        for st in self.streams.values():
            for o in reversed(st):
                if o.fn is not None and o.dma is None:
                    lasts.append(o)
                    break
        lasts += list(self.dma_last.values())
        for d in lasts:
            d.needed = True
        for e in self.ENGS:
            o = _Op(e, None)
            o.deps = tuple(lasts)
            self.streams[e].append(o)
        self.lastw.clear()
        self.readers.clear()

    def emit(self):
        nc = self.nc
        esem = {}
        for e in self.ENGS:
            esem[e] = self.stack.enter_context(nc.semaphore("es_" + e))
        for e in self.ENGS:
            cnt = 0
            for o in self.streams[e]:
                if o.dma is not None:
                    ent = self.dma_sems.get(o.dma)
                    if ent is None:
                        ent = [self.stack.enter_context(nc.semaphore("ds_" + str(o.dma))), 0]
                        self.dma_sems[o.dma] = ent
                    ent[1] += o.inc
                    o.sem = ent[0]
                    o.val = ent[1]
                elif o.needed:
                    cnt += 1
                    o.sem = esem[e]
                    o.val = cnt
        streams = self.streams

        def run(engname, eng):
            waited = {}
            for o in streams[engname]:
                for d in o.deps:
                    sid = id(d.sem)
                    if waited.get(sid, 0) < d.val:
                        waited[sid] = d.val
                        eng.wait_ge(d.sem, d.val)
                if o.fn is None:
                    continue
                ins = o.fn(eng)
                if o.dma is not None:
                    ins.then_inc(o.sem, o.inc)
                elif o.needed:
                    ins.then_inc(o.sem, 1)

        with nc.Block() as block:
            @block.tensor
            def _(eng):
                run("pe", eng)

            @block.vector
            def _(eng):
                run("dve", eng)

            @block.scalar
            def _(eng):
                run("act", eng)

            @block.gpsimd
            def _(eng):
                run("pool", eng)

            @block.sync
            def _(eng):
                run("sp", eng)


def _mm_group(out, pairs):
    n = len(pairs)

    def fn(pe):
        ins = None
        for i, (l, r) in enumerate(pairs):
            ins = pe.matmul(out, l, r, start=(i == 0), stop=(i == n - 1))
        return ins
    return fn


def body_post(p, io, NT, D, F, final, FG=256, fz=None):
    KC = D // 128
    TP = 512
    NPASS = NT // TP
    TT = TP // 128
    NFC = FG // 128
    NG = F // FG
    DB = D // 512
    OC = 256
    x_d, yT_d, wo_d, wu_d, wd_d, g_d, out_d = io["x"], io.get("yT"), io["w_out"], io["w_up"], io["w_down"], io["g_mlp"], io["out"]
    gf_d = io.get("g_fin")
    acc = p.sb("acc", [128, TT, D], F32)
    hT = p.sb("hT", [128, KC, TP], BF16)
    xn = p.sb("xn", [128, D], BF16)
    junk = p.sb("junk", [128, D], BF16)
    w1 = [p.sb(f"w1_{i}", [128, KC, FG], BF16) for i in range(2)]
    w2 = [p.sb(f"w2_{i}", [128, NFC, D], BF16) for i in range(2)]
    uT = [p.sb(f"uT_{i}", [128, NFC, TP], BF16) for i in range(2)]
    sq = [p.sb(f"sq_{i}", [128, TP], F32) for i in range(2)]
    gT = p.sb("gT", [128, KC], F32)
    ident = p.sb("ident", [128, 128], BF16)
    identf = p.sb("identf", [128, 128], F32)
    ssum = p.sb("ssum", [128, 1], F32)
    ms = p.sb("ms", [128, 1], F32)
    rstd = p.sb("rstd", [128, 1], F32)
    if final:
        gfb = p.sb("gfb", [128, D], F32)
    ps_up = [(p.ps(f"psu{i}", [128, 512]), f"psu{i}") for i in range(2)]
    ps_dn = [(p.ps(f"psd{i}", [128, 512]), f"psd{i}") for i in range(4)]
    ps_tr = [(p.ps(f"pst{i}", [128, 1024], BF16), f"pst{i}") for i in range(2)]

    p.op("pool", lambda e: e.memset(identf[:], 1.0), w=["identf"])
    p.op("pool", lambda e: e.affine_select(out=identf[:], in_=identf[:], pattern=[[1, 128]],
                                            compare_op=ALU.is_equal, fill=0.0, base=0,
                                            channel_multiplier=-1), r=["identf"], w=["identf"])
    p.op("dve", lambda e: e.tensor_copy(out=ident[:], in_=identf[:]), r=["identf"], w=["ident"])
    p.dma("sp", lambda e: e.dma_start(out=gT[:], in_=g_d), w=["gTm"], sem="gT")
    if final:
        p.dma("sp", lambda e: e.dma_start(out=gfb[:], in_=gf_d.partition_broadcast(128)), w=["gfb"], sem="gfb")

    wd_v = wd_d.rearrange("(c p) n -> p c n", p=128)
    yT_v = yT_d.rearrange("(k p) t -> p k t", p=128) if yT_d is not None else None

    wslot = 0
    for ps_ in range(NPASS):
        t0 = ps_ * TP
        for tt in range(TT):
            p.dma("sp", lambda e, tt=tt, t0=t0: e.dma_start(out=acc[:, tt, :], in_=x_d[t0 + tt * 128:t0 + (tt + 1) * 128, :]),
                  w=[f"acc{tt}_{db}" for db in range(DB)], sem=f"acc{tt}")
        if fz is None:
            p.dma("act", lambda e, t0=t0: e.dma_start(out=hT[:], in_=yT_v[:, :, t0:t0 + TP]),
                  w=[f"hT{tt}" for tt in range(TT)], sem="hT")
        else:
            fz["load_hT"](ps_, hT, [f"hT{tt}" for tt in range(TT)])
        for ob in range(D // OC):
            s = wslot % 2
            wslot += 1
            wt = w1[s]
            p.dma("pool", lambda e, wt=wt, ob=ob: e.dma_start(out=wt[:, :, 0:OC], in_=wo_d[ob]),
                  w=[f"w1_{s}"], sem=f"w1_{s}")
            for tt in range(TT):
                pd, pdk = ps_dn[(ob * TT + tt) % 4]
                pairs = [(hT[:, k, tt * 128:(tt + 1) * 128], wt[:, k, 0:OC]) for k in range(KC)]
                p.op("pe", _mm_group(pd[:, 0:OC], pairs), r=[f"hT{tt}", f"w1_{s}"], w=[pdk])
                db = (ob * OC) // 512
                p.op("dve", lambda e, pd=pd, tt=tt, ob=ob: e.tensor_tensor(
                    out=acc[:, tt, ob * OC:(ob + 1) * OC], in0=pd[:, 0:OC],
                    in1=acc[:, tt, ob * OC:(ob + 1) * OC], op=ALU.add),
                    r=[pdk], w=[f"acc{tt}_{db}"])
        for tt in range(TT):
            _rms_tile(p, acc, tt, DB, hT, gT, xn, junk, ssum, ms, rstd, ident, ps_tr, D)
        for g in range(NG):
            s = g % 2
            w1t, w2t, uTt = w1[s], w2[s], uT[s]
            p.dma("pool", lambda e, w1t=w1t, g=g: e.dma_start(out=w1t[:], in_=wu_d[g]), w=[f"w1_{s}"], sem=f"w1_{s}")
            for fc in range(NFC):
                p.dma("pool", lambda e, w2t=w2t, g=g, fc=fc: e.dma_start(
                    out=w2t[:, fc, :], in_=wd_v[:, g * NFC + fc, :]),
                    w=[f"w2_{s}_{fc}"], sem=f"w2_{s}_{fc}")
            for fc in range(NFC):
                pu, puk = ps_up[(g * NFC + fc) % 2]
                sqt = sq[(g * NFC + fc) % 2]
                sqk = f"sq{(g * NFC + fc) % 2}"
                pairs = [(w1t[:, k, fc * 128:(fc + 1) * 128], hT[:, k, :]) for k in range(KC)]
                p.op("pe", _mm_group(pu[:, 0:TP], pairs), r=[f"w1_{s}"] + [f"hT{tt}" for tt in range(TT)], w=[puk])
                p.op("act", lambda e, pu=pu, sqt=sqt: e.activation(out=sqt[:], in_=pu[:, 0:TP], func=AF.Square),
                     r=[puk], w=[sqk])
                p.op("dve", lambda e, pu=pu, sqt=sqt, uTt=uTt, fc=fc: e.scalar_tensor_tensor(
                    out=uTt[:, fc, :], in0=pu[:, 0:TP], scalar=0.0, in1=sqt[:], op0=ALU.is_gt, op1=ALU.mult),
                    r=[puk, sqk], w=[f"uT{s}_{fc}"])
            for tt in range(TT):
                for db in range(DB):
                    pd, pdk = ps_dn[(tt * DB + db) % 4]
                    pairs = [(uTt[:, fc, tt * 128:(tt + 1) * 128], w2t[:, fc, db * 512:(db + 1) * 512]) for fc in range(NFC)]
                    p.op("pe", _mm_group(pd[:, :], pairs),
                         r=[f"uT{s}_{fc}" for fc in range(NFC)] + [f"w2_{s}_{fc}" for fc in range(NFC)], w=[pdk])
                    p.op("dve", lambda e, pd=pd, tt=tt, db=db: e.tensor_tensor(
                        out=acc[:, tt, db * 512:(db + 1) * 512], in0=pd[:, :],
                        in1=acc[:, tt, db * 512:(db + 1) * 512], op=ALU.add),
                        r=[pdk], w=[f"acc{tt}_{db}"])
        if fz is not None and "after_pass" in fz:
            fz["after_pass"](ps_, acc, hT, xn, junk, ssum, ms, rstd, ident, ps_tr, TT, DB)
        for tt in range(TT):
            acck = [f"acc{tt}_{db}" for db in range(DB)]
            if final:
                p.op("act", lambda e, tt=tt: e.activation(out=junk[:], in_=acc[:, tt, :], func=AF.Square, accum_out=ssum[:]),
                     r=acck, w=["junk", "ssum"])
                p.op("dve", lambda e: e.tensor_scalar(out=ms[:], in0=ssum[:], scalar1=1.0 / D, scalar2=NORM_EPS,
                                                      op0=ALU.mult, op1=ALU.add), r=["ssum"], w=["ms"])
                p.op("act", lambda e: e.activation(out=ms[:], in_=ms[:], func=AF.Sqrt), r=["ms"], w=["ms"])
                p.op("dve", lambda e: e.reciprocal(out=rstd[:], in_=ms[:]), r=["ms"], w=["rstd"])
                p.op("dve", lambda e, tt=tt: e.scalar_tensor_tensor(
                    out=acc[:, tt, :], in0=acc[:, tt, :], scalar=rstd[:], in1=gfb[:], op0=ALU.mult, op1=ALU.mult),
                    r=acck + ["rstd", "gfb"], w=acck)
            p.dma("sp", lambda e, tt=tt, t0=t0: e.dma_start(out=out_d[t0 + tt * 128:t0 + (tt + 1) * 128, :], in_=acc[:, tt, :]),
                  r=acck, sem=f"acc{tt}")


    return TT, DB


def build_post(NT=1024, D=D_MODEL, F=D_FF, final=False, FG=256):
    nc = bass.Bass("TRN2", target_bir_lowering=False)
    KC = D // 128
    io = {
        "x": nc.dram_tensor("x", [NT, D], F32, kind="ExternalInput").ap(),
        "yT": nc.dram_tensor("yT", [D, NT], BF16, kind="ExternalInput").ap(),
        "w_out": nc.dram_tensor("w_out", [D // 256, 128, KC, 256], F32, kind="ExternalInput").ap(),
        "w_up": nc.dram_tensor("w_up", [F // FG, 128, KC, FG], F32, kind="ExternalInput").ap(),
        "w_down": nc.dram_tensor("w_down", [F, D], F32, kind="ExternalInput").ap(),
        "g_mlp": nc.dram_tensor("g_mlp", [128, KC], F32, kind="ExternalInput").ap(),
    }
    if final:
        io["g_fin"] = nc.dram_tensor("g_fin", [1, D], F32, kind="ExternalInput").ap()
    io["out"] = nc.dram_tensor("out", [NT, D], F32, kind="ExternalOutput").ap()
    with ExitStack() as stack:
        p = Prog(nc, stack)
        TT, DB = body_post(p, io, NT, D, F, final, FG)
        p.op("sp", None, w=[f"acc{tt}_{db}" for tt in range(TT) for db in range(DB)])
        p.emit()
    return nc


def _rms_tile(p, acc, tt, DB, hT, gT, xn, junk, ssum, ms, rstd, ident, ps_tr, D, gkey="gTm"):
    KC = D // 128
    acck = [f"acc{tt}_{db}" for db in range(DB)]
    src = acc[:, tt, :]
    p.op("act", lambda e: e.activation(out=junk[:], in_=src, func=AF.Square, accum_out=ssum[:]),
         r=acck, w=["junk", "ssum"])
    p.op("dve", lambda e: e.tensor_scalar(out=ms[:], in0=ssum[:], scalar1=1.0 / D, scalar2=NORM_EPS,
                                          op0=ALU.mult, op1=ALU.add), r=["ssum"], w=["ms"])
    p.op("act", lambda e: e.activation(out=ms[:], in_=ms[:], func=AF.Sqrt), r=["ms"], w=["ms"])
    p.op("dve", lambda e: e.reciprocal(out=rstd[:], in_=ms[:]), r=["ms"], w=["rstd"])
    p.op("act", lambda e: e.activation(out=xn[:], in_=src, func=AF.Copy, scale=rstd[:]),
         r=acck + ["rstd"], w=["xn"])
    for k0 in range(0, KC, 8):
        nk = min(8, KC - k0)
        pst, pst_key = ps_tr[(k0 // 8) % len(ps_tr)]

        def fn(pe, k0=k0, nk=nk, pst=pst):
            ins = None
            for kk in range(nk):
                ins = pe.transpose(pst[:, kk * 128:(kk + 1) * 128], xn[:, (k0 + kk) * 128:(k0 + kk + 1) * 128], ident[:])
            return ins
        p.op("pe", fn, r=["xn", "ident"], w=[pst_key])
        p.op("dve", lambda e, k0=k0, nk=nk, pst=pst: e.tensor_tensor(
            out=hT[:, k0:k0 + nk, tt * 128:(tt + 1) * 128],
            in0=pst[:, 0:nk * 128].rearrange("p (k t) -> p k t", t=128),
            in1=gT[:, k0:k0 + nk].unsqueeze(2).to_broadcast([128, nk, 128]),
            op=ALU.mult), r=[pst_key, gkey], w=[f"hT{tt}"])


def o_tt(p, eng, out, a, b, op, r, w):
    return p.op(eng, lambda e: e.tensor_tensor(out=out, in0=a, in1=b, op=op), r, w)


def o_ts(p, eng, out, a, s1, s2, op0, op1, r, w):
    if s2 is None:
        return p.op(eng, lambda e: e.tensor_scalar(out=out, in0=a, scalar1=s1, scalar2=None, op0=op0), r, w)
    return p.op(eng, lambda e: e.tensor_scalar(out=out, in0=a, scalar1=s1, scalar2=s2, op0=op0, op1=op1), r, w)


def o_stt(p, out, a, sc, b, op0, op1, r, w):
    return p.op("dve", lambda e: e.scalar_tensor_tensor(out=out, in0=a, scalar=sc, in1=b, op0=op0, op1=op1), r, w)


def o_act(p, out, in_, func, r, w, bias=None, scale=None, accum=None):
    kw = {}
    if bias is not None:
        kw["bias"] = bias
    if scale is not None:
        kw["scale"] = scale
    if accum is not None:
        kw["accum_out"] = accum
    return p.op("act", lambda e: e.activation(out=out, in_=in_, func=func, **kw), r, w)


def o_cp(p, eng, out, in_, r, w):
    if eng == "act":
        return p.op("act", lambda e: e.activation(out=out, in_=in_, func=AF.Copy), r, w)
    return p.op(eng, lambda e: e.tensor_copy(out=out, in_=in_), r, w)


def o_mm(p, out, pairs, r, w):
    return p.op("pe", _mm_group(out, pairs), r, w)


def o_tr(p, out, in_, ident, r, w):
    return p.op("pe", lambda e: e.transpose(out, in_, ident), r, w)


def o_scan(p, out, d0, d1, init, op0, op1, r, w):
    return p.op("dve", lambda e: e.tensor_tensor_scan(out=out, data0=d0, data1=d1, initial=init, op0=op0, op1=op1), r, w)


def o_memset(p, eng, ap, val, w):
    return p.op(eng, lambda e: e.memset(ap, val), (), w)


def o_asel(p, out, in_, pattern, cmp, fill, base, cm, r, w):
    return p.op("pool", lambda e: e.affine_select(out=out, in_=in_, pattern=pattern, compare_op=cmp, fill=fill,
                                                   base=base, channel_multiplier=cm), r, w)


def bc(ap, shape):
    return ap.to_broadcast(list(shape))


def emit_consts(p):
    c = {}
    identf = p.sb("identf", [128, 128], F32)
    ident = p.sb("ident", [128, 128], BF16)
    onesf = p.sb("onesf", [128, 128], F32)
    o_memset(p, "pool", onesf[:], 1.0, ["onesf"])
    o_asel(p, identf[:], onesf[:], [[1, 128]], ALU.is_equal, 0.0, 0, -1, ["onesf"], ["identf"])
    o_cp(p, "dve", ident[:], identf[:], ["identf"], ["ident"])
    c["identf"], c["ident"], c["onesf"] = identf, ident, onesf
    return c


def rms_to_hT(p, src, src_keys, hT, hT_key, tok0, gT, xn, junk, st, ident, ps_tr, D):
    KC = D // 128
    ssum, ms, rstd = st
    o_act(p, junk[:], src, AF.Square, src_keys, ["junk", "ssum"], accum=ssum[:])
    o_ts(p, "dve", ms[:], ssum[:], 1.0 / D, NORM_EPS, ALU.mult, ALU.add, ["ssum"], ["ms"])
    o_act(p, ms[:], ms[:], AF.Sqrt, ["ms"], ["ms"])
    p.op("dve", lambda e: e.reciprocal(out=rstd[:], in_=ms[:]), ["ms"], ["rstd"])
    o_act(p, xn[:], src, AF.Copy, list(src_keys) + ["rstd"], ["xn"], scale=rstd[:])
    for k0 in range(0, KC, 8):
        nk = min(8, KC - k0)
        pst, pst_key = ps_tr[(k0 // 8) % len(ps_tr)]

        def fn(pe, k0=k0, nk=nk, pst=pst):
            ins = None
            for kk in range(nk):
                ins = pe.transpose(pst[:, kk * 128:(kk + 1) * 128], xn[:, (k0 + kk) * 128:(k0 + kk + 1) * 128], ident[:])
            return ins
        p.op("pe", fn, r=["xn", "ident"], w=[pst_key])
        o_tt(p, "dve", hT[:, k0:k0 + nk, tok0:tok0 + 128],
             pst[:, 0:nk * 128].rearrange("p (k t) -> p k t", t=128),
             bc(gT[:, k0:k0 + nk].unsqueeze(2), [128, nk, 128]), ALU.mult, [pst_key, "gTm"], [hT_key])


class Arena:
    def __init__(self, p, name, nbytes):
        self.t = p.sb(name, [128, nbytes // 4], F32)
        self.n = nbytes // 4
        self.off = 0
        self.name = name
        self.cnt = 0

    def reset(self):
        self.off = 0

    def f32(self, shape):
        n = int(np.prod(shape[1:]))
        assert self.off + n <= self.n, ("arena overflow", self.off + n, self.n)
        ap = self.t[0:shape[0], self.off:self.off + n]
        self.off += n
        if len(shape) > 2:
            names = " ".join(f"d{i}" for i in range(len(shape) - 1))
            ap = ap.rearrange(f"p ({names}) -> p {names}", **{f"d{i}": shape[i + 1] for i in range(len(shape) - 1)})
        return ap

    def bf16(self, shape):
        n = int(np.prod(shape[1:]))
        n32 = (n + 1) // 2
        assert self.off + n32 <= self.n, ("arena overflow", self.off + n32, self.n)
        ap = self.t[0:shape[0], self.off:self.off + n32].bitcast(BF16)[:, 0:n]
        self.off += n32
        if len(shape) > 2:
            names = " ".join(f"d{i}" for i in range(len(shape) - 1))
            ap = ap.rearrange(f"p ({names}) -> p {names}", **{f"d{i}": shape[i + 1] for i in range(len(shape) - 1)})
        return ap


LRU_C = 8.0
GN_EPS = 64e-5
NCOL_A = 5568
DECAY_SCALE = 0.6065306597126334


def body_mixA(p, nc, io, S, D):
    KC = D // 128
    TB = 512
    NB = S // TB
    L = 64
    NCH = TB // L
    x_d, g_d, w_d, lp_d, wa_d, wx_d = io["x"], io["g_mix"], io["w_in"], io["lru_p"], io["lru_wa"], io["lru_wx"]
    rp_d, rl_d, w2_d, a2_d, g2_d, ln_d, yT_d = io["rw_p"], io["rw_lmu"], io["rw_w2"], io["rw_a2"], io["rw_g2"], io["rw_ln"], io["yT"]
    C = emit_consts(p)
    ident, identf, onesf = C["ident"], C["identf"], C["onesf"]
    hT = p.sb("hT", [128, KC, TB], BF16)
    wbuf = [p.sb(f"wb{i}", [128, KC, 128], BF16) for i in range(4)]
    gT = p.sb("gT", [128, KC], F32)
    lp = p.sb("lp", [128, 8, 8], F32)
    rp = p.sb("rp", [128, 8, 8], F32)
    rl = p.sb("rl", [128, 4], F32)
    wa = p.sb("wa", [128, 4, 2, 256], BF16)
    wx = p.sb("wx", [128, 4, 2, 256], BF16)
    w2 = p.sb("w2", [128, 1024], BF16)
    a2 = p.sb("a2", [128, 1024], BF16)
    g2 = p.sb("g2", [128, 2, 1024], BF16)
    lnp = p.sb("lnp", [128, 2, 8, 64], F32)
    cl8 = p.sb("cl8", [128, 8], F32)
    cl16 = p.sb("cl16", [128, 8], F32)
    omka = p.sb("omka", [128, 8], F32)
    bmask = p.sb("bmask", [128, 2], F32)
    m_strict = p.sb("m_strict", [128, 2, 64], F32)
    m_incl = p.sb("m_incl", [128, 2, 64], F32)
    m_nincl = p.sb("m_nincl", [128, 2, 64], F32)
    m_lower = p.sb("m_lower", [128, 2, 64], F32)
    ones_bd = p.sb("ones_bd", [128, 2, 64], F32)
    istack = p.sb("istack", [128, 64], BF16)
    ident_bd = p.sb("ident_bd", [128, 128], F32)
    rmask = p.sb("rmask", [128, NCH, L], F32)
    ones512 = p.sb("ones512", [128, TB], F32)
    onescol = p.sb("onescol", [128, 1], BF16)
    S32 = p.sb("S32", [128, 8, 64], F32)
    STb = p.sb("STb", [128, 8, 64], BF16)
    hst = p.sb("hst", [128, 8], F32)
    halo_u = p.sb("halo_u", [128, 8, 3], F32)
    halo_z = p.sb("halo_z", [128, 28], F32)
    st = (p.sb("ssum", [128, 1], F32), p.sb("ms", [128, 1], F32), p.sb("rstd", [128, 1], F32))
    arena = Arena(p, "arena", 104 * 1024)
    pz = [(p.ps(f"pz{i}", [128, 512]), f"pz{i}") for i in range(2)]
    pg = [(p.ps(f"pg{i}", [128, 512]), f"pg{i}") for i in range(5)]
    ptr = [(p.ps("ptr0", [128, 1024], BF16), "ptr0")]

    p.dma("sp", lambda e: e.dma_start(out=gT[:], in_=g_d), w=["gTm"], sem="c0")
    p.dma("sp", lambda e: e.dma_start(out=lp[:], in_=lp_d), w=["lp"], sem="c0")
    p.dma("sp", lambda e: e.dma_start(out=rp[:], in_=rp_d), w=["rp"], sem="c0")
    p.dma("sp", lambda e: e.dma_start(out=rl[:], in_=rl_d), w=["rl"], sem="c0")
    p.dma("sp", lambda e: e.dma_start(out=lnp[:], in_=ln_d), w=["lnp"], sem="c0")
    p.dma("pool", lambda e: e.dma_start(out=wa[:], in_=wa_d.rearrange("h (c q) j -> q h c j", q=128)), w=["wa"], sem="c1")
    p.dma("pool", lambda e: e.dma_start(out=wx[:], in_=wx_d.rearrange("h (c q) j -> q h c j", q=128)), w=["wx"], sem="c1")
    p.dma("pool", lambda e: e.dma_start(out=w2[0:96, :], in_=w2_d), w=["w2"], sem="c1")
    p.dma("pool", lambda e: e.dma_start(out=a2[0:96, :], in_=a2_d), w=["a2"], sem="c1")
    p.dma("pool", lambda e: e.dma_start(out=g2[:], in_=g2_d.rearrange("(c q) j -> q c j", q=128)), w=["g2"], sem="c1")
    p.barrier()
    o_act(p, cl8[:], lp[:, :, 7], AF.Exp, ["lp"], ["cl8"], scale=-1.0)
    o_act(p, cl8[:], cl8[:], AF.Ln, ["cl8"], ["cl8"], bias=1.0)
    o_ts(p, "dve", cl16[:], cl8[:], -2.0 * LRU_C, None, ALU.mult, None, ["cl8"], ["cl16"])
    o_ts(p, "dve", cl8[:], cl8[:], -LRU_C, None, ALU.mult, None, ["cl8"], ["cl8"])
    o_ts(p, "dve", omka[:], rp[:, :, 6], -1.0, 1.0, ALU.mult, ALU.add, ["rp"], ["omka"])
    o_memset(p, "pool", ones512[:], 1.0, ["ones512"])
    o_memset(p, "dve", onescol[:], 1.0, ["onescol"])
    o_asel(p, bmask[:, 0:1], onesf[:, 0:1], [[0, 1]], ALU.is_ge, 0.0, 63, -1, ["onesf"], ["bmask"])
    o_asel(p, bmask[:, 1:2], onesf[:, 0:1], [[0, 1]], ALU.is_ge, 0.0, -64, 1, ["onesf"], ["bmask"])
    o_asel(p, m_strict[:, 0, :], onesf[:, 0:64], [[1, 64]], ALU.is_ge, 0.0, -1, -1, ["onesf"], ["m_strict"])
    o_asel(p, m_strict[:, 1, :], onesf[:, 0:64], [[1, 64]], ALU.is_ge, 0.0, 63, -1, ["onesf"], ["m_strict"])
    o_asel(p, m_strict[:, 1, :], m_strict[:, 1, :], [[0, 64]], ALU.is_ge, 0.0, -64, 1, ["m_strict"], ["m_strict"])
    o_asel(p, m_incl[:, 0, :], onesf[:, 0:64], [[1, 64]], ALU.is_ge, 0.0, 0, -1, ["onesf"], ["m_incl"])
    o_asel(p, m_incl[:, 1, :], onesf[:, 0:64], [[1, 64]], ALU.is_ge, 0.0, 64, -1, ["onesf"], ["m_incl"])
    o_asel(p, m_incl[:, 1, :], m_incl[:, 1, :], [[0, 64]], ALU.is_ge, 0.0, -64, 1, ["m_incl"], ["m_incl"])
    o_ts(p, "dve", m_nincl[:], m_incl[:], -1.0, None, ALU.mult, None, ["m_incl"], ["m_nincl"])
    o_asel(p, m_lower[:, 0, :], onesf[:, 0:64], [[-1, 64]], ALU.is_ge, 0.0, -1, 1, ["onesf"], ["m_lower"])
    o_asel(p, m_lower[:, 0, :], m_lower[:, 0, :], [[0, 64]], ALU.is_ge, 0.0, 63, -1, ["m_lower"], ["m_lower"])
    o_asel(p, m_lower[:, 1, :], onesf[:, 0:64], [[-1, 64]], ALU.is_ge, 0.0, -65, 1, ["onesf"], ["m_lower"])
    o_tt(p, "dve", ones_bd[:], bc(bmask[:].unsqueeze(2), [128, 2, 64]), bc(bmask[:].unsqueeze(2), [128, 2, 64]), ALU.mult,
         ["bmask"], ["ones_bd"])
    o_tt(p, "dve", istack[:], identf[:, 0:64], identf[:, 64:128], ALU.add, ["identf"], ["istack"])
    o_cp(p, "dve", ident_bd[:], identf[:], ["identf"], ["ident_bd"])
    o_asel(p, rmask[:], ones512[:].rearrange("p (c t) -> p c t", t=L),
           [[0, NCH], [1, L]], ALU.not_equal, 0.0, 0, 0, ["ones512"], ["rmask"])
    o_memset(p, "dve", S32[:], 0.0, ["S32"])
    o_memset(p, "dve", STb[:], 0.0, ["STb"])
    o_memset(p, "dve", hst[:], 0.0, ["hst"])
    o_memset(p, "dve", halo_u[:], 0.0, ["halo_u"])
    o_memset(p, "dve", halo_z[:], 0.0, ["halo_z"])
    p.barrier()

    wstate = {"i": 0}

    def load_w(tile, ncols):
        s = wstate["i"] % 4
        wstate["i"] += 1
        wt = wbuf[s]
        p.dma("pool", lambda e: e.dma_start(out=wt[:], in_=w_d[tile]), w=[f"wb{s}"], sem=f"wb{s}")
        return wt, f"wb{s}"

    zi = {"i": 0}

    def inproj(col0, ncols):
        wt, wk = load_w(col0, ncols)
        pzt, pzk = pz[zi["i"] % 2]
        zi["i"] += 1
        pairs = [(wt[:, k, 0:ncols], hT[:, k, :]) for k in range(KC)]
        o_mm(p, pzt[0:ncols, :], pairs, [wk] + [f"hT{tt}" for tt in range(4)], [pzk])
        return pzt[0:ncols, :], pzk

    for b in range(NB):
        t0 = b * TB
        arena.reset()
        xs = arena.f32([128, D])
        xn = arena.bf16([128, D])
        junk = arena.bf16([128, D])
        for tt in range(4):
            p.dma("sp", lambda e, tt=tt, t0=t0: e.dma_start(out=xs, in_=x_d[t0 + tt * 128:t0 + (tt + 1) * 128, :]),
                  w=["xs"], sem="xs")
            _rms_generic(p, xs, ["xs"], hT, f"hT{tt}", tt * 128, gT, xn, junk, st, ident, ptr, D)
        p.barrier()
        arena.reset()
        ub = [arena.f32([128, TB + 3]) for _ in range(2)]
        uc = [arena.f32([128, TB]) for _ in range(2)]
        ucb = [arena.bf16([128, TB]) for _ in range(2)]
        rr = arena.f32([128, TB])
        ii = arena.f32([128, TB])
        aa = arena.f32([128, TB])
        bb = arena.f32([128, TB])
        hh_ = arena.f32([128, TB])
        gg = arena.f32([128, TB])
        t1 = arena.f32([128, TB])
        yo = arena.bf16([128, TB])
        for h in range(4):
            for c2 in range(2):
                c = 2 * h + c2
                pu, puk = inproj(h * 4 + c2, 128)
                ubt = ub[c2]
                o_cp(p, "dve", ubt[:, 0:3], halo_u[:, c, :], ["halo_u"], [f"ub{c2}"])
                o_cp(p, "act", ubt[:, 3:TB + 3], pu, [puk], [f"ub{c2}"])
                o_cp(p, "dve", halo_u[:, c, :], ubt[:, TB:TB + 3], [f"ub{c2}"], ["halo_u"])
                o_ts(p, "dve", uc[c2], ubt[:, 3:TB + 3], lp[:, c, 3:4], lp[:, c, 4:5], ALU.mult, ALU.add,
                     [f"ub{c2}", "lp"], [f"uc{c2}"])
                for j in range(3):
                    o_stt(p, uc[c2], ubt[:, j:TB + j], lp[:, c, j:j + 1], uc[c2], ALU.mult, ALU.add,
                          [f"ub{c2}", "lp", f"uc{c2}"], [f"uc{c2}"])
                o_cp(p, "act", ucb[c2], uc[c2], [f"uc{c2}"], [f"ucb{c2}"])
            for c2 in range(2):
                c = 2 * h + c2
                pr, prk = pg[0]
                pi, pik = pg[1]
                o_mm(p, pr[:, :], [(wa[:, h, ic, c2 * 128:(c2 + 1) * 128], ucb[ic]) for ic in range(2)],
                     ["wa", "ucb0", "ucb1"], [prk])
                o_mm(p, pi[:, :], [(wx[:, h, ic, c2 * 128:(c2 + 1) * 128], ucb[ic]) for ic in range(2)],
                     ["wx", "ucb0", "ucb1"], [pik])
                o_act(p, rr, pr[:, :], AF.Sigmoid, [prk, "lp"], ["rr"], bias=lp[:, c, 5:6])
                o_act(p, ii, pi[:, :], AF.Sigmoid, [pik, "lp"], ["ii"], bias=lp[:, c, 6:7])
                o_act(p, aa, rr, AF.Exp, ["rr", "cl8"], ["aa"], scale=cl8[:, c:c + 1])
                o_act(p, bb, rr, AF.Exp, ["rr", "cl16"], ["bb"], scale=cl16[:, c:c + 1])
                o_ts(p, "dve", bb, bb, -1.0, 1.0, ALU.mult, ALU.add, ["bb"], ["bb"])
                o_act(p, bb, bb, AF.Sqrt, ["bb"], ["bb"])
                o_tt(p, "pool", ii, ii, uc[c2], ALU.mult, ["ii", f"uc{c2}"], ["ii"])
                o_tt(p, "dve", bb, bb, ii, ALU.mult, ["bb", "ii"], ["bb"])
                o_scan(p, hh_, aa, bb, hst[:, c:c + 1], ALU.mult, ALU.add, ["aa", "bb", "hst"], ["hh"])
                o_cp(p, "dve", hst[:, c:c + 1], hh_[:, TB - 1:TB], ["hh"], ["hst"])
                pgz, pgk = inproj(h * 4 + 2 + c2, 128)
                o_cp(p, "act", gg, pgz, [pgk], ["gg"])
                o_act(p, t1, pgz, AF.Square, [pgk], ["t1"])
                o_ts(p, "dve", t1, t1, 0.044715, 1.0, ALU.mult, ALU.add, ["t1"], ["t1"])
                o_tt(p, "dve", t1, t1, gg, ALU.mult, ["t1", "gg"], ["t1"])
                o_act(p, t1, t1, AF.Sigmoid, ["t1"], ["t1"], scale=1.5957691216057308)
                o_tt(p, "pool", gg, gg, t1, ALU.mult, ["gg", "t1"], ["gg"])
                o_tt(p, "dve", yo, gg, hh_, ALU.mult, ["gg", "hh"], ["yo"])
                p.dma("sp", lambda e, c=c, t0=t0: e.dma_start(out=yT_d[c * 128:(c + 1) * 128, t0:t0 + TB], in_=yo),
                      r=["yo"], sem="yo")
        p.barrier()
        arena.reset()
        _rwkv_block(p, nc, arena, b, t0, TB, NCH, L, inproj, pg, ptr, dict(
            rp=rp, rl=rl, w2=w2, a2=a2, g2=g2, lnp=lnp, omka=omka, bmask=bmask, m_strict=m_strict, m_incl=m_incl,
            m_nincl=m_nincl, m_lower=m_lower, ones_bd=ones_bd, istack=istack, ident_bd=ident_bd, rmask=rmask,
            ones512=ones512, onescol=onescol, S32=S32, STb=STb, halo_z=halo_z, ident=ident, identf=identf), yT_d)
        p.barrier()


def decl_mixA(nc, S, D, sfx="", out_kind="ExternalOutput"):
    KC = D // 128
    dt = nc.dram_tensor
    io = {
        "x": dt("x" + sfx, [S, D], F32, kind="ExternalInput").ap(),
        "g_mix": dt("g_mix" + sfx, [128, KC], F32, kind="ExternalInput").ap(),
        "w_in": dt("w_in" + sfx, [44, 128, KC, 128], F32, kind="ExternalInput").ap(),
        "lru_p": dt("lru_p", [128, 8, 8], F32, kind="ExternalInput").ap(),
        "lru_wa": dt("lru_wa", [4, 256, 256], F32, kind="ExternalInput").ap(),
        "lru_wx": dt("lru_wx", [4, 256, 256], F32, kind="ExternalInput").ap(),
        "rw_p": dt("rw_p", [128, 8, 8], F32, kind="ExternalInput").ap(),
        "rw_lmu": dt("rw_lmu", [128, 4], F32, kind="ExternalInput").ap(),
        "rw_w2": dt("rw_w2", [96, 1024], F32, kind="ExternalInput").ap(),
        "rw_a2": dt("rw_a2", [96, 1024], F32, kind="ExternalInput").ap(),
        "rw_g2": dt("rw_g2", [256, 1024], F32, kind="ExternalInput").ap(),
        "rw_ln": dt("rw_ln", [128, 2, 8, 64], F32, kind="ExternalInput").ap(),
    }
    if out_kind is not None:
        io["yT"] = dt("yT", [2048, S], BF16, kind=out_kind).ap()
    return io


def build_mixA(S=SEQ, D=D_MODEL):
    nc = bass.Bass("TRN2", target_bir_lowering=False)
    io = decl_mixA(nc, S, D)
    with ExitStack() as stack:
        p = Prog(nc, stack)
        body_mixA(p, nc, io, S, D)
        p.op("sp", None, w=["yo", "yout"])
        p.emit()
    return nc


def _rms_generic(p, xs, xs_keys, hT, hT_key, tok0, gT, xn, junk, st, ident, ps_tr, D):
    KC = D // 128
    ssum, ms, rstd = st
    o_act(p, junk, xs, AF.Square, xs_keys, ["junk", "ssum"], accum=ssum[:])
    o_ts(p, "dve", ms[:], ssum[:], 1.0 / D, NORM_EPS, ALU.mult, ALU.add, ["ssum"], ["ms"])
    o_act(p, ms[:], ms[:], AF.Sqrt, ["ms"], ["ms"])
    p.op("dve", lambda e: e.reciprocal(out=rstd[:], in_=ms[:]), ["ms"], ["rstd"])
    o_act(p, xn, xs, AF.Copy, list(xs_keys) + ["rstd"], ["xn"], scale=rstd[:])
    for k0 in range(0, KC, 8):
        nk = min(8, KC - k0)
        pst, pst_key = ps_tr[(k0 // 8) % len(ps_tr)]

        def fn(pe, k0=k0, nk=nk, pst=pst):
            ins = None
            for kk in range(nk):
                ins = pe.transpose(pst[:, kk * 128:(kk + 1) * 128], xn[:, (k0 + kk) * 128:(k0 + kk + 1) * 128], ident[:])
            return ins
        p.op("pe", fn, r=["xn", "ident"], w=[pst_key])
        o_tt(p, "dve", hT[:, k0:k0 + nk, tok0:tok0 + 128],
             pst[:, 0:nk * 128].rearrange("p (k t) -> p k t", t=128),
             bc(gT[:, k0:k0 + nk].unsqueeze(2), [128, nk, 128]), ALU.mult, [pst_key, "gTm"], [hT_key])


def _rwkv_block(p, nc, ar, b, t0, TB, NCH, L, inproj, pg, ptr, K, yT_d):
    rp, rl, w2, a2, g2, lnp, omka, bmask = K["rp"], K["rl"], K["w2"], K["a2"], K["g2"], K["lnp"], K["omka"], K["bmask"]
    m_strict, m_incl, m_nincl, m_lower = K["m_strict"], K["m_incl"], K["m_nincl"], K["m_lower"]
    ones_bd, istack, ident_bd, rmask, onescol = K["ones_bd"], K["istack"], K["ident_bd"], K["rmask"], K["onescol"]
    S32, STb, halo_z, ident = K["S32"], K["STb"], K["halo_z"], K["ident"]
    f32, bf = ar.f32, ar.bf16
    zl = f32([128, 4, TB + 1])
    tw = bf([128, TB])
    alb = bf([128, TB])
    sg = bf([128, 2, TB])
    ltmp = f32([128, TB])
    for li, (col0, n) in enumerate([(16, 96), (17, 96), (18, 128), (19, 128)]):
        pzt, pzk = inproj(col0, n)
        hz = halo_z[0:n, 24 + li:25 + li]
        o_cp(p, "dve", zl[0:n, li, 0:1], hz, ["halo_z"], [f"zl{li}"])
        o_cp(p, "act", zl[0:n, li, 1:TB + 1], pzt, [pzk], [f"zl{li}"])
        o_cp(p, "dve", hz, zl[0:n, li, TB:TB + 1], [f"zl{li}"], ["halo_z"])
        o_tt(p, "pool", ltmp[0:n, :], zl[0:n, li, 0:TB], zl[0:n, li, 1:TB + 1], ALU.subtract, [f"zl{li}"], ["ltmp"])
        o_stt(p, ltmp[0:n, :], ltmp[0:n, :], rl[0:n, li:li + 1], zl[0:n, li, 1:TB + 1], ALU.mult, ALU.add,
              ["ltmp", "rl", f"zl{li}"], ["ltmp"])
        if li == 0:
            o_act(p, tw[0:n, :], ltmp[0:n, :], AF.Tanh, ["ltmp"], ["tw"])
        elif li == 1:
            o_cp(p, "act", alb[0:n, :], ltmp[0:n, :], ["ltmp"], ["alb"])
        else:
            o_act(p, sg[:, li - 2, :], ltmp[:, :], AF.Sigmoid, ["ltmp"], [f"sg{li - 2}"])

    z3 = f32([128, 3, TB + 1])
    rkv = f32([128, 3, TB])
    sig = f32([128, TB])
    aa = f32([128, TB])
    gsb = f32([128, TB])
    kk = f32([128, TB])
    t1 = f32([128, TB])
    t2 = f32([128, TB])
    kap = f32([128, TB])
    kmod = f32([128, TB])
    csum = f32([128, TB])
    epos = f32([128, TB])
    eneg = f32([128, TB])
    eprev = f32([128, TB])
    KR = bf([128, NCH, 2 * L])
    btp = bf([128, TB])
    ktp = bf([128, TB])
    rkk = bf([128, TB])
    vb = bf([128, TB])
    bd_names = ["bt", "kapt", "kt", "rt", "v", "rkk"]
    bd = {n: bf([128, NCH, 2 * L]) for n in bd_names}
    XTt = f32([128, NCH, 2, 2 * L])
    XTr = f32([128, NCH, 2 * L])
    Mkk = bf([128, NCH, 2 * L])
    Mkr = bf([128, NCH, 2 * L])
    nMbr = bf([128, NCH, 2 * L])
    Tbd = bf([128, NCH, 2 * L])
    ktok = bf([128, NCH, 2 * L])
    nbtok = bf([128, NCH, 2 * L])
    vst = bf([128, NCH, L])
    bs = f32([128, NCH])
    upre = bf([128, L])
    usb = bf([128, L])
    s32p = f32([128, L])
    ysb = f32([128, NCH, L])
    yc = f32([128, NCH, L])
    mean = f32([128, NCH])
    var = f32([128, NCH])
    yfb = bf([128, NCH, 2 * L])
    yout = bf([128, TB])

    def v4(ap3):
        return ap3.rearrange("p c (h t) -> p c h t", h=2)

    def to_bd(eng, dst, src, n, rk, wk):
        o_tt(p, eng, v4(dst), bc(src.rearrange("p (c t) -> p c t", t=L).unsqueeze(2), [128, n, 2, L]),
             bc(bmask[:].unsqueeze(1).unsqueeze(3), [128, n, 2, L]), ALU.mult, rk + ["bmask"], wk)

    for q in range(8):
        for j in range(3):
            pzt, pzk = inproj(20 + q * 3 + j, 128)
            hz = halo_z[:, 3 * q + j:3 * q + j + 1]
            o_cp(p, "dve", z3[:, j, 0:1], hz, ["halo_z"], [f"z3{j}"])
            o_cp(p, "act", z3[:, j, 1:TB + 1], pzt, [pzk], [f"z3{j}"])
            o_cp(p, "dve", hz, z3[:, j, TB:TB + 1], [f"z3{j}"], ["halo_z"])
            o_tt(p, "pool", t1, z3[:, j, 0:TB], z3[:, j, 1:TB + 1], ALU.subtract, [f"z3{j}"], ["t1"])
            o_stt(p, rkv[:, j, :], t1, rp[:, q, j:j + 1], z3[:, j, 1:TB + 1], ALU.mult, ALU.add,
                  ["t1", "rp", f"z3{j}"], [f"rkv{j}"])
        r_, k_, v_ = rkv[:, 0, :], rkv[:, 1, :], rkv[:, 2, :]
        qs = slice(q * 128, (q + 1) * 128)
        px, pxk = pg[0]
        o_mm(p, px[:, :], [(w2[0:96, qs], tw[0:96, :])], ["w2", "tw"], [pxk])
        o_act(p, sig, px[:, :], AF.Sigmoid, [pxk, "rp"], ["sig"], bias=rp[:, q, 3:4])
        pa, pak = pg[1]
        o_mm(p, pa[:, :], [(a2[0:96, qs], alb[0:96, :])], ["a2", "alb"], [pak])
        o_act(p, aa, pa[:, :], AF.Sigmoid, [pak, "rp"], ["aa"], bias=rp[:, q, 4:5])
        pgt, pgk = pg[2]
        o_mm(p, pgt[:, :], [(g2[:, c, qs], sg[:, c, :]) for c in range(2)], ["g2", "sg0", "sg1"], [pgk])
        o_cp(p, "act", gsb, pgt[:, :], [pgk], ["gsb"])
        o_ts(p, "dve", kk, k_, rp[:, q, 5:6], None, ALU.mult, None, ["rkv1", "rp"], ["kk"])
        o_tt(p, "pool", t1, kk, kk, ALU.mult, ["kk"], ["t1"])
        pn, pnk = pg[3]
        o_mm(p, pn[:, :], [(ones_bd[:].rearrange("p a b -> p (a b)"), t1)], ["ones_bd", "t1"], [pnk])
        o_act(p, t2, pn[:, :], AF.Sqrt, [pnk], ["t2"])
        o_ts(p, "dve", t2, t2, 1e-12, None, ALU.max, None, ["t2"], ["t2"])
        p.op("dve", lambda e: e.reciprocal(out=t2, in_=t2), ["t2"], ["t2"])
        o_tt(p, "dve", kap, kk, t2, ALU.mult, ["kk", "t2"], ["kap"])
        o_ts(p, "dve", t1, aa, rp[:, q, 6:7], omka[:, q:q + 1], ALU.mult, ALU.add, ["aa", "rp", "omka"], ["t1"])
        o_tt(p, "pool", kmod, k_, t1, ALU.mult, ["rkv1", "t1"], ["kmod"])
        o_tt(p, "pool", t2, kap, aa, ALU.mult, ["kap", "aa"], ["t2"])
        o_scan(p, csum, rmask[:].rearrange("p c t -> p (c t)"), sig, 0.0, ALU.mult, ALU.add, ["rmask", "sig"], ["csum"])
        o_act(p, epos, csum, AF.Exp, ["csum"], ["epos"], scale=-DECAY_SCALE)
        o_act(p, eneg, csum, AF.Exp, ["csum"], ["eneg"], scale=DECAY_SCALE)
        o_tt(p, "dve", t1, csum, sig, ALU.subtract, ["csum", "sig"], ["t1"])
        o_act(p, eprev, t1, AF.Exp, ["t1"], ["eprev"], scale=-DECAY_SCALE)
        KR4 = KR.rearrange("p c (x t) -> p c x t", x=2)
        o_tt(p, "dve", KR4[:, :, 0, :], kap.rearrange("p (c t) -> p c t", t=L), eprev.rearrange("p (c t) -> p c t", t=L),
             ALU.mult, ["kap", "eprev"], ["KR0"])
        o_tt(p, "dve", KR4[:, :, 1, :], r_.rearrange("p (c t) -> p c t", t=L), epos.rearrange("p (c t) -> p c t", t=L),
             ALU.mult, ["rkv0", "epos"], ["KR1"])
        o_tt(p, "pool", btp, t2, eneg, ALU.mult, ["t2", "eneg"], ["btp"])
        o_tt(p, "pool", ktp, kmod, eneg, ALU.mult, ["kmod", "eneg"], ["ktp"])
        o_stt(p, rkk, r_, rp[:, q, 7:8], kmod, ALU.mult, ALU.mult, ["rkv0", "rp", "kmod"], ["rkk"])
        o_cp(p, "act", vb, v_, ["rkv2"], ["vb"])
        o_tt(p, "dve", v4(bd["kapt"]), bc(KR4[:, :, 0:1, :], [128, NCH, 2, L]),
             bc(bmask[:].unsqueeze(1).unsqueeze(3), [128, NCH, 2, L]), ALU.mult, ["KR0", "bmask"], ["bd_kapt"])
        o_tt(p, "pool", v4(bd["rt"]), bc(KR4[:, :, 1:2, :], [128, NCH, 2, L]),
             bc(bmask[:].unsqueeze(1).unsqueeze(3), [128, NCH, 2, L]), ALU.mult, ["KR1", "bmask"], ["bd_rt"])
        to_bd("dve", bd["bt"], btp, NCH, ["btp"], ["bd_bt"])
        to_bd("pool", bd["kt"], ktp, NCH, ["ktp"], ["bd_kt"])
        to_bd("dve", bd["v"], vb, NCH, ["vb"], ["bd_v"])
        to_bd("pool", bd["rkk"], rkk, NCH, ["rkk"], ["bd_rkk"])
        pt, ptk = ptr[0]
        p.op("pe", lambda e: [e.transpose(pt[:, c * 128:(c + 1) * 128], bd["kt"][:, c, :], ident[:]) for c in range(NCH)][-1],
             ["bd_kt", "ident"], [ptk])
        o_cp(p, "act", ktok.rearrange("p c t -> p (c t)"), pt[:, :], [ptk], ["ktok"])
        p.op("pe", lambda e: [e.transpose(pt[:, c * 128:(c + 1) * 128], bd["bt"][:, c, :], ident[:]) for c in range(NCH)][-1],
             ["bd_bt", "ident"], [ptk])
        o_ts(p, "dve", nbtok.rearrange("p c t -> p (c t)"), pt[:, :], -1.0, None, ALU.mult, None, [ptk], ["nbtok"])
        pv, pvk = pg[2]

        def fn_vst(e, pv=pv):
            ins = None
            for c in range(NCH):
                ins = e.matmul(pv[:, c * L:(c + 1) * L], bd["v"][:, c, :], istack[:], start=True, stop=True)
            return ins
        p.op("pe", fn_vst, ["bd_v", "istack"], [pvk])
        o_cp(p, "act", vst.rearrange("p c t -> p (c t)"), pv[:, :], [pvk], ["vst"])
        pb_, pbk = pg[3]

        def fn_bs(e, pb_=pb_):
            ins = None
            for c in range(NCH):
                ins = e.matmul(pb_[:, c:c + 1], bd["rkk"][:, c, :], onescol[:], start=True, stop=True)
            return ins
        p.op("pe", fn_bs, ["bd_rkk", "onescol"], [pbk])
        o_cp(p, "dve", bs, pb_[:, 0:NCH], [pbk], ["bs"])
        for cb in range(NCH // 2):
            c0 = 2 * cb
            cs = slice(c0, c0 + 2)
            P1, P1k = pg[0]
            P2, P2k = pg[1]
            P3, P3k = pg[2]
            P1v = P1[:, 0:4 * L].rearrange("p (c x t) -> p c x t", c=2, x=2)
            P2v = P2[:, 0:4 * L].rearrange("p (c x t) -> p c x t", c=2, x=2)
            P3v = P3[:, 0:2 * L].rearrange("p (c t) -> p c t", c=2)
            msk = lambda m: bc(m[:].unsqueeze(1), [128, 2, 2, L])

            def fn1(e, c0=c0, P1=P1):
                ins = None
                for cc in range(2):
                    ins = e.matmul(P1[:, cc * 128:(cc + 1) * 128], bd["bt"][:, c0 + cc, :], KR[:, c0 + cc, :], start=True, stop=True)
                return ins
            p.op("pe", fn1, ["bd_bt", "KR0", "KR1"], [P1k])
            o_tt(p, "dve", v4(XTt[:, cs, 0, :]), bc(P1v[:, :, 0:1, :], [128, 2, 2, L]), msk(m_strict), ALU.mult,
                 [P1k, "m_strict"], [f"X{cb}"])
            o_tt(p, "dve", v4(nMbr[:, cs, :]), bc(P1v[:, :, 1:2, :], [128, 2, 2, L]), msk(m_nincl), ALU.mult,
                 [P1k, "m_nincl"], ["nMbr"])

            def fn2(e, c0=c0, P2=P2):
                ins = None
                for cc in range(2):
                    ins = e.matmul(P2[:, cc * 128:(cc + 1) * 128], bd["kt"][:, c0 + cc, :], KR[:, c0 + cc, :], start=True, stop=True)
                return ins
            p.op("pe", fn2, ["bd_kt", "KR0", "KR1"], [P2k])
            o_tt(p, "dve", v4(Mkk[:, cs, :]), bc(P2v[:, :, 0:1, :], [128, 2, 2, L]), msk(m_strict), ALU.mult,
                 [P2k, "m_strict"], ["Mkk"])
            o_tt(p, "dve", v4(Mkr[:, cs, :]), bc(P2v[:, :, 1:2, :], [128, 2, 2, L]), msk(m_incl), ALU.mult,
                 [P2k, "m_incl"], ["Mkr"])

            def fn3(e, c0=c0, P3=P3):
                ins = None
                for cc in range(2):
                    ins = e.matmul(P3[:, cc * L:(cc + 1) * L], bd["kapt"][:, c0 + cc, :], btp[:, (c0 + cc) * L:(c0 + cc + 1) * L],
                                   start=True, stop=True)
                return ins
            p.op("pe", fn3, ["bd_kapt", "btp"], [P3k])
            o_tt(p, "dve", v4(XTr[:, cs, :]), bc(P3v.unsqueeze(2), [128, 2, 2, L]), msk(m_lower), ALU.mult,
                 [P3k, "m_lower"], [f"XT{cb}"])
            o_tt(p, "pool", XTt[:, cs, 1, :], bc(ident_bd[:].unsqueeze(1), [128, 2, 128]), XTt[:, cs, 0, :], ALU.subtract,
                 ["ident_bd", f"X{cb}"], [f"T{cb}"])
            for lvl in range(1, 7):
                PA, PAk = pg[0]
                PB, PBk = pg[1]
                need_sq = lvl <= 4
                need_sqT = lvl <= 5
                need_prod = lvl >= 2
                if need_sq or need_prod:
                    lo = 0 if need_sq else 128
                    hi = 256 if need_prod else 128

                    def fnA(e, c0=c0, PA=PA, lo=lo, hi=hi):
                        ins = None
                        for cc in range(2):
                            ins = e.matmul(PA[:, cc * 256 + lo:cc * 256 + hi], XTr[:, c0 + cc, :],
                                           XTt[:, c0 + cc, :, :].rearrange("p x t -> p (x t)")[:, lo:hi], start=True, stop=True)
                        return ins
                    rk = [f"XT{cb}"] + ([f"X{cb}"] if need_sq else []) + ([f"T{cb}"] if need_prod else [])
                    p.op("pe", fnA, rk, [PAk])
                if need_sqT:
                    def fnB(e, c0=c0, PB=PB):
                        ins = None
                        for cc in range(2):
                            ins = e.matmul(PB[:, cc * 128:(cc + 1) * 128], XTt[:, c0 + cc, 0, :], XTr[:, c0 + cc, :], start=True, stop=True)
                        return ins
                    p.op("pe", fnB, [f"X{cb}", f"XT{cb}"], [PBk])
                PAv = PA[:, :].rearrange("p (c x t) -> p c x t", c=2, x=2)
                if need_prod:
                    o_tt(p, "dve", XTt[:, cs, 1, :], PAv[:, :, 1, :], XTt[:, cs, 1, :], ALU.add, [PAk, f"T{cb}"], [f"T{cb}"])
                if need_sq:
                    o_cp(p, "act", XTt[:, cs, 0, :], PAv[:, :, 0, :], [PAk], [f"X{cb}"])
                if need_sqT:
                    o_cp(p, "act", XTr[:, cs, :], PB[:, 0:256].rearrange("p (c t) -> p c t", c=2), [PBk], [f"XT{cb}"])
            o_cp(p, "pool", Tbd[:, cs, :], XTt[:, cs, 1, :], [f"T{cb}"], ["Tbd"])
        pU, pUk = pg[3]
        pY, pYk = pg[4]
        for c in range(NCH):
            o_mm(p, pU[:, 0:L], [(bd["kapt"][:, c, :], STb[:, q, :]), (Mkk[:, c, :], vst[:, c, :])],
                 ["bd_kapt", "STb", "Mkk", "vst"], [pUk])
            o_cp(p, "act", upre, pU[:, 0:L], [pUk], ["upre"])
            o_mm(p, pU[:, L:2 * L], [(Tbd[:, c, :], upre)], ["Tbd", "upre"], [pUk])
            o_cp(p, "dve", usb, pU[:, L:2 * L], [pUk], ["usb"])
            o_mm(p, pY[:, c * L:(c + 1) * L], [(bd["rt"][:, c, :], STb[:, q, :]), (Mkr[:, c, :], vst[:, c, :]), (nMbr[:, c, :], usb)],
                 ["bd_rt", "STb", "Mkr", "vst", "nMbr", "usb"], [pYk])
            o_mm(p, pU[:, 2 * L:3 * L], [(ktok[:, c, :], vst[:, c, :]), (nbtok[:, c, :], usb)],
                 ["ktok", "vst", "nbtok", "usb"], [pUk])
            pl = epos[:, c * L + L - 1:c * L + L]
            o_ts(p, "pool", s32p, S32[:, q, :], pl, None, ALU.mult, None, ["S32", "epos"], ["s32p"])
            o_stt(p, S32[:, q, :], pU[:, 2 * L:3 * L], pl, s32p, ALU.mult, ALU.add, [pUk, "epos", "s32p"], ["S32"])
            o_cp(p, "act", STb[:, q, :], S32[:, q, :], ["S32"], ["STb"])
        o_cp(p, "act", ysb, pY[:, :].rearrange("p (c t) -> p c t", t=L), [pYk], ["ysb"])
        p.op("dve", lambda e: e.tensor_reduce(out=mean, in_=ysb, axis=AX.X, op=ALU.add), ["ysb"], ["mean"])
        o_ts(p, "dve", mean, mean, 1.0 / L, None, ALU.mult, None, ["mean"], ["mean"])
        o_tt(p, "dve", yc, ysb, bc(mean.unsqueeze(2), [128, NCH, L]), ALU.subtract, ["ysb", "mean"], ["yc"])
        o_tt(p, "pool", ysb, yc, yc, ALU.mult, ["yc"], ["ysb"])
        p.op("dve", lambda e: e.tensor_reduce(out=var, in_=ysb, axis=AX.X, op=ALU.add), ["ysb"], ["var"])
        o_ts(p, "dve", var, var, 1.0 / L, GN_EPS, ALU.mult, ALU.add, ["var"], ["var"])
        o_act(p, var, var, AF.Sqrt, ["var"], ["var"])
        p.op("dve", lambda e: e.reciprocal(out=var, in_=var), ["var"], ["var"])
        o_tt(p, "dve", yc, yc, bc(var.unsqueeze(2), [128, NCH, L]), ALU.mult, ["yc", "var"], ["yc"])
        o_tt(p, "dve", yc, yc, bc(lnp[:, 0, q, :].unsqueeze(1), [128, NCH, L]), ALU.mult, ["yc", "lnp"], ["yc"])
        o_tt(p, "dve", yc, yc, bc(lnp[:, 1, q, :].unsqueeze(1), [128, NCH, L]), ALU.add, ["yc", "lnp"], ["yc"])
        o_tt(p, "dve", ysb, vst, bc(bs.unsqueeze(2), [128, NCH, L]), ALU.mult, ["vst", "bs"], ["ysb"])
        o_tt(p, "dve", yc, yc, ysb, ALU.add, ["yc", "ysb"], ["yc"])
        o_tt(p, "dve", v4(yfb), bc(yc.unsqueeze(2), [128, NCH, 2, L]), bc(bmask[:].unsqueeze(1).unsqueeze(3), [128, NCH, 2, L]),
             ALU.mult, ["yc", "bmask"], ["yfb"])
        pF, pFk = pg[2]

        def fn_f(e, pF=pF):
            ins = None
            for c in range(NCH):
                ins = e.matmul(pF[:, c * L:(c + 1) * L], yfb[:, c, :], istack[:], start=True, stop=True)
            return ins
        p.op("pe", fn_f, ["yfb", "istack"], [pFk])
        o_tt(p, "dve", yout, pF[:, :], gsb, ALU.mult, [pFk, "gsb"], ["yout"])
        p.dma("sp", lambda e, q=q, t0=t0: e.dma_start(out=yT_d[1024 + q * 128:1024 + (q + 1) * 128, t0:t0 + TB], in_=yout),
              r=["yout"], sem="yout")


NCOL_C = 6152
ML_CH = 64


def body_mixC(p, nc, io, S, D, fz=None):
    KC = D // 128
    TB = 512
    NB = S // TB
    L = ML_CH
    NCH = TB // L
    NT = TB // 128
    x_d, g_d, w_d, gb_d, nw_d, yT_d = io.get("x"), io["g_mix"], io["w_in"], io["gate_b"], io["norm_w"], io["yT"]
    C = emit_consts(p)
    ident, identf, onesf = C["ident"], C["identf"], C["onesf"]
    hT = p.sb("hT", [128, KC, TB], BF16)
    wbuf = [p.sb(f"wb{i}", [128, KC, 128], BF16) for i in range(4)]
    gT = p.sb("gT", [128, KC], F32)
    gb = p.sb("gb", [4, 2], F32)
    nwb = p.sb("nwb", [128, 2048], F32)
    cmask = p.sb("cmask", [128, 2, 64], F32)
    onescol = p.sb("onescol", [128, 1], BF16)
    ones512 = p.sb("ones512", [4, TB], F32)
    zeros512 = p.sb("zeros512", [4, TB], F32)
    C32 = p.sb("C32", [128, 4, 2, 512], F32)
    Cbf = p.sb("Cbf", [128, 4, 2, 512], BF16)
    n32 = p.sb("n32", [128, 4, 2], F32)
    nbf = p.sb("nbf", [128, 4, 2], BF16)
    carry = p.sb("carry", [4, 2], F32)
    st = (p.sb("ssum", [128, 1], F32), p.sb("ms", [128, 1], F32), p.sb("rstd", [128, 1], F32))
    arena = Arena(p, "arena", 96 * 1024)
    pz = [(p.ps(f"pz{i}", [128, 512]), f"pz{i}") for i in range(2)]
    pg = [(p.ps(f"pg{i}", [128, 512]), f"pg{i}") for i in range(5)]
    ptr = [(p.ps("ptr0", [128, 1024], BF16), "ptr0")]

    p.dma("sp", lambda e: e.dma_start(out=gT[:], in_=g_d), w=["gTm"], sem="c0")
    p.dma("sp", lambda e: e.dma_start(out=gb[:], in_=gb_d), w=["gb"], sem="c0")
    p.dma("sp", lambda e: e.dma_start(out=nwb[:], in_=nw_d.partition_broadcast(128)), w=["nwb"], sem="c0")
    p.barrier()
    o_ts(p, "dve", gb[:, 1:2], gb[:, 1:2], -1.0, None, ALU.mult, None, ["gb"], ["gb"])
    o_memset(p, "dve", onescol[:], 1.0, ["onescol"])
    o_memset(p, "dve", ones512[:], 1.0, ["ones512"])
    o_memset(p, "dve", zeros512[:], 0.0, ["zeros512"])
    o_memset(p, "dve", C32[:], 0.0, ["C32"])
    o_memset(p, "pool", Cbf[:], 0.0, ["Cbf"])
    o_memset(p, "dve", n32[:], 0.0, ["n32"])
    o_memset(p, "dve", nbf[:], 0.0, ["nbf"])
    o_memset(p, "dve", carry[:], 0.0, ["carry"])
    o_asel(p, cmask[:, 0, :], onesf[:, 0:64], [[1, 64]], ALU.is_ge, 0.0, 0, -1, ["onesf"], ["cmask"])
    o_asel(p, cmask[:, 1, :], onesf[:, 0:64], [[1, 64]], ALU.is_ge, 0.0, 64, -1, ["onesf"], ["cmask"])
    o_asel(p, cmask[:, 1, :], cmask[:, 1, :], [[0, 64]], ALU.is_ge, 0.0, -64, 1, ["cmask"], ["cmask"])
    p.barrier()

    wstate = {"i": 0}

    def load_w(tile, ncols):
        s = wstate["i"] % 4
        wstate["i"] += 1
        wt = wbuf[s]
        p.dma("pool", lambda e: e.dma_start(out=wt[:], in_=w_d[tile]), w=[f"wb{s}"], sem=f"wb{s}")
        return wt, f"wb{s}"

    zi = {"i": 0}
    hTk = [f"hT{tt}" for tt in range(NT)]

    def inproj_fm(col0, ncols):
        wt, wk = load_w(col0, ncols)
        pzt, pzk = pz[zi["i"] % 2]
        zi["i"] += 1
        o_mm(p, pzt[0:ncols, :], [(wt[:, k, 0:ncols], hT[:, k, :]) for k in range(KC)], [wk] + hTk, [pzk])
        return pzt[0:ncols, :], pzk

    def inproj_tm(col0, nchunks, evac):
        wts = [load_w(col0 + cc, 128) for cc in range(nchunks)]
        for tt in range(NT):
            pzt, pzk = pz[zi["i"] % 2]
            zi["i"] += 1

            def fn(e, tt=tt, pzt=pzt):
                ins = None
                for cc in range(nchunks):
                    wt = wts[cc][0]
                    for k in range(KC):
                        ins = e.matmul(pzt[:, cc * 128:(cc + 1) * 128], hT[:, k, tt * 128:(tt + 1) * 128], wt[:, k, :],
                                       start=(k == 0), stop=(k == KC - 1))
                return ins
            p.op("pe", fn, [w[1] for w in wts] + [f"hT{tt}"], [pzk])
            evac(tt, pzt[:, 0:nchunks * 128], pzk)

    for b in range(NB):
        t0 = b * TB
        arena.reset()
        xs = arena.f32([128, D])
        xn = arena.bf16([128, D])
        junk = arena.bf16([128, D])
        if fz is None:
            for tt in range(NT):
                p.dma("sp", lambda e, tt=tt, t0=t0: e.dma_start(out=xs, in_=x_d[t0 + tt * 128:t0 + (tt + 1) * 128, :]),
                      w=["xs"], sem="xs")
                _rms_generic(p, xs, ["xs"], hT, f"hT{tt}", tt * 128, gT, xn, junk, st, ident, ptr, D)
        else:
            fz["load_hT"](b, hT, hTk)
        p.barrier()
        arena.reset()
        f32, bf = arena.f32, arena.bf16
        li = f32([4, TB])
        sp_ = f32([4, TB])
        cs = f32([4, TB])
        G = f32([4, TB])
        Mg = f32([4, TB])
        muv = f32([4, NCH])
        colfac = f32([4, TB])
        dfl = f32([4, TB])
        dch = f32([4, NCH])
        dsel = f32([4, 4, NCH])
        colT = f32([128, NT, 4])
        colTb = bf([128, NT, 4])
        dflT = f32([128, NT, 4])
        dbc = f32([128, 4, NCH])
        pi_, pik = inproj_fm(48, 4)
        o_act(p, li, pi_, AF.Identity, [pik, "gb"], ["li"], bias=gb[:, 0:1])
        pf_, pfk = inproj_fm(49, 4)
        o_act(p, sp_, pf_, AF.Exp, [pfk, "gb"], ["sp"], bias=gb[:, 1:2], scale=-1.0)
        o_act(p, sp_, sp_, AF.Ln, ["sp"], ["sp"], bias=1.0)
        o_scan(p, cs, ones512[:], sp_, carry[:, 0:1], ALU.mult, ALU.add, ["ones512", "sp", "carry"], ["cs"])
        o_tt(p, "dve", G, li, cs, ALU.add, ["li", "cs"], ["G"])
        o_scan(p, Mg, G, zeros512[:], carry[:, 1:2], ALU.max, ALU.max, ["G", "zeros512", "carry"], ["Mg"])
        Mg3 = Mg.rearrange("p (c t) -> p c t", t=L)
        o_cp(p, "dve", muv[:, 0:1], carry[:, 1:2], ["carry"], ["muv"])
        o_cp(p, "dve", muv[:, 1:NCH], Mg3[:, 0:NCH - 1, L - 1], ["Mg"], ["muv"])
        o_tt(p, "dve", dch, muv, Mg3[:, :, L - 1], ALU.subtract, ["muv", "Mg"], ["dch"])
        o_act(p, dch, dch, AF.Exp, ["dch"], ["dch"])
        o_tt(p, "dve", colfac.rearrange("p (c t) -> p c t", t=L), G.rearrange("p (c t) -> p c t", t=L),
             bc(muv.unsqueeze(2), [4, NCH, L]), ALU.subtract, ["G", "muv"], ["colfac"])
        o_act(p, colfac, colfac, AF.Exp, ["colfac"], ["colfac"])
        o_tt(p, "dve", dfl.rearrange("p (c t) -> p c t", t=L), cs.rearrange("p (c t) -> p c t", t=L),
             bc(muv.unsqueeze(2), [4, NCH, L]), ALU.subtract, ["cs", "muv"], ["dfl"])
        o_act(p, dfl, dfl, AF.Exp, ["dfl"], ["dfl"])
        o_cp(p, "dve", carry[:, 0:1], cs[:, TB - 1:TB], ["cs"], ["carry"])
        o_cp(p, "dve", carry[:, 1:2], Mg[:, TB - 1:TB], ["Mg"], ["carry"])
        pq, pqk = pg[0]

        def fn_t(e, src=colfac, off=0):
            ins = None
            for tt in range(NT):
                ins = e.transpose(pq[:, off + tt * 4:off + (tt + 1) * 4], src[:, tt * 128:(tt + 1) * 128], identf[0:4, 0:4])
            return ins
        p.op("pe", fn_t, ["colfac", "identf"], [pqk])
        o_cp(p, "dve", colT.rearrange("p a b -> p (a b)"), pq[:, 0:NT * 4], [pqk], ["colT"])
        o_cp(p, "act", colTb.rearrange("p a b -> p (a b)"), pq[:, 0:NT * 4], [pqk], ["colTb"])
        p.op("pe", lambda e: fn_t(e, dfl, 0), ["dfl", "identf"], [pqk])
        o_cp(p, "dve", dflT.rearrange("p a b -> p (a b)"), pq[:, 0:NT * 4], [pqk], ["dflT"])
        o_tt(p, "dve", dsel, bc(dch.unsqueeze(1), [4, 4, NCH]), bc(identf[0:4, 0:4].unsqueeze(2), [4, 4, NCH]), ALU.mult,
             ["dch", "identf"], ["dsel"])
        o_mm(p, pq[:, 0:4 * NCH], [(onesf[0:4, :], dsel.rearrange("p a b -> p (a b)"))], ["onesf", "dsel"], [pqk])
        o_cp(p, "dve", dbc.rearrange("p a b -> p (a b)"), pq[:, 0:4 * NCH], [pqk], ["dbc"])
        qT = [bf([128, 2, TB]) for _ in range(4)]
        kT = [bf([128, 2, TB]) for _ in range(4)]
        ktok = [bf([128, NT, 256]) for _ in range(4)]
        vtk = [bf([128, NT, 512]) for _ in range(4)]
        gw = [bf([128, NT, 512]) for _ in range(4)]
        hsT = [bf([128, 4, TB]) for _ in range(4)]
        sgt = f32([128, 512])
        ATs = bf([128, 128])
        vhat = bf([128, 512])
        hs = bf([128, 512])
        sm = f32([128, 8])
        for h in range(4):
            for dc in range(2):
                pq_, pqk_ = inproj_fm(h * 4 + dc, 128)
                o_act(p, qT[h][:, dc, :], pq_, AF.Copy, [pqk_], [f"qT{h}"], scale=0.0625)
            for dc in range(2):
                pk_, pkk_ = inproj_fm(h * 4 + 2 + dc, 128)
                o_cp(p, "act", kT[h][:, dc, :], pk_, [pkk_], [f"kT{h}"])
            inproj_tm(h * 4 + 2, 2, lambda tt, ps, k_, h=h: o_cp(p, "dve", ktok[h][:, tt, :], ps, [k_], [f"ktok{h}"]))
            inproj_tm(16 + h * 4, 4, lambda tt, ps, k_, h=h: o_cp(p, "act", vtk[h][:, tt, :], ps, [k_], [f"vtk{h}"]))

            def ev_o(tt, ps, k_, h=h):
                o_act(p, sgt, ps, AF.Sigmoid, [k_], ["sgt"])
                o_tt(p, "dve", gw[h][:, tt, :], sgt, nwb[:, h * 512:(h + 1) * 512], ALU.mult, ["sgt", "nwb"], [f"gw{h}"])
            inproj_tm(32 + h * 4, 4, ev_o)
        pA, pAk = pg[0]
        pN, pNk = pg[1]
        pD, pDk = pg[2]
        pC = [pg[3], pg[4]]
        for tt in range(NT):
            tsl = slice(tt * 128, (tt + 1) * 128)
            for h in range(4):
                o_mm(p, pA[:, 0:128], [(kT[h][:, dc, tsl], qT[h][:, dc, tsl]) for dc in range(2)], [f"kT{h}", f"qT{h}"], [pAk])
                o_stt(p, ATs, pA[:, 0:128], colT[:, tt, h:h + 1], cmask[:].rearrange("p a b -> p (a b)"), ALU.mult, ALU.mult,
                      [pAk, "colT", "cmask"], ["ATs"])
                o_act(p, vhat, vtk[h][:, tt, :], AF.Copy, [f"vtk{h}", "colT"], ["vhat"], scale=colT[:, tt, h:h + 1])
                p.op("pe", lambda e, h=h, tt=tt: e.matmul(pN[:, :], ATs, vtk[h][:, tt, :], start=True, stop=False,
                                                          skip_group_check=True), ["ATs", f"vtk{h}"], [pNk])
                p.op("pe", lambda e: e.matmul(pD[:, 0:1], ATs, onescol[:], start=True, stop=False, skip_group_check=True),
                     ["ATs", "onescol"], [pDk])
                for cc in range(2):
                    c = tt * 2 + cc
                    rows = slice(cc * 64, (cc + 1) * 64)
                    csl = slice(tt * 128 + cc * 64, tt * 128 + (cc + 1) * 64)

                    def fn_state(e, h=h, rows=rows, csl=csl):
                        ins = None
                        for dc in range(2):
                            ins = e.matmul(pN[rows, :], qT[h][:, dc, csl], Cbf[:, h, dc, :], start=False, stop=(dc == 1),
                                           skip_group_check=True)
                        for dc in range(2):
                            ins = e.matmul(pD[rows, 0:1], qT[h][:, dc, csl], nbf[:, h, dc:dc + 1], start=False, stop=(dc == 1),
                                           skip_group_check=True)
                        return ins
                    p.op("pe", fn_state, [f"qT{h}", f"Cbf{h}", f"nbf{h}"], [pNk, pDk])
                    dsc = dbc[:, h, c % NCH:c % NCH + 1]
                    for dc in range(2):
                        pc, pck = pC[dc]
                        o_mm(p, pc[:, :], [(ktok[h][rows, tt, dc * 128:(dc + 1) * 128], vhat[rows, :])],
                             [f"ktok{h}", "vhat"], [pck])
                        o_ts(p, "pool", C32[:, h, dc, :], C32[:, h, dc, :], dsc, None, ALU.mult, None,
                             [f"C32{h}{dc}", "dbc"], [f"C32{h}{dc}"])
                        o_stt(p, C32[:, h, dc, :], pc[:, :], dsc, C32[:, h, dc, :], ALU.mult, ALU.add,
                              [pck, "dbc", f"C32{h}{dc}"], [f"C32{h}{dc}"])
                        o_cp(p, "act", Cbf[:, h, dc, :], C32[:, h, dc, :], [f"C32{h}{dc}"], [f"Cbf{h}"])
                    pn_, pnk_ = pg[0]

                    def fn_n(e, h=h, rows=rows, tt=tt):
                        ins = None
                        for dc in range(2):
                            ins = e.matmul(pn_[:, 200 + dc:201 + dc], ktok[h][rows, tt, dc * 128:(dc + 1) * 128],
                                           colTb[rows, tt, h:h + 1], start=True, stop=True)
                        return ins
                    p.op("pe", fn_n, [f"ktok{h}", "colTb"], [pnk_])
                    o_ts(p, "pool", n32[:, h, :], n32[:, h, :], dsc, None, ALU.mult, None, [f"n32{h}", "dbc"], [f"n32{h}"])
                    o_stt(p, n32[:, h, :], pn_[:, 200:202], dsc, n32[:, h, :], ALU.mult, ALU.add,
                          [pnk_, "dbc", f"n32{h}"], [f"n32{h}"])
                    o_cp(p, "dve", nbf[:, h, :], n32[:, h, :], [f"n32{h}"], [f"nbf{h}"])
                o_act(p, sgt, pN[:, :], AF.Square, [pNk], ["sgt", "sm0"], accum=sm[:, 0:1])
                o_act(p, sm[:, 1:2], pD[:, 0:1], AF.Abs, [pDk], ["sm1"])
                o_tt(p, "dve", sm[:, 1:2], sm[:, 1:2], dflT[:, tt, h:h + 1], ALU.max, ["sm1", "dflT"], ["sm1"])
                p.op("dve", lambda e: e.reciprocal(out=sm[:, 2:3], in_=sm[:, 1:2]), ["sm1"], ["sm2"])
                o_tt(p, "dve", sm[:, 3:4], sm[:, 2:3], sm[:, 2:3], ALU.mult, ["sm2"], ["sm3"])
                o_tt(p, "dve", sm[:, 3:4], sm[:, 3:4], sm[:, 0:1], ALU.mult, ["sm3", "sm0"], ["sm3"])
                o_ts(p, "dve", sm[:, 3:4], sm[:, 3:4], 1.0 / 512, NORM_EPS, ALU.mult, ALU.add, ["sm3"], ["sm3"])
                o_act(p, sm[:, 3:4], sm[:, 3:4], AF.Sqrt, ["sm3"], ["sm3"])
                p.op("dve", lambda e: e.reciprocal(out=sm[:, 4:5], in_=sm[:, 3:4]), ["sm3"], ["sm4"])
                o_tt(p, "dve", sm[:, 4:5], sm[:, 4:5], sm[:, 2:3], ALU.mult, ["sm4", "sm2"], ["sm4"])
                o_stt(p, hs, pN[:, :], sm[:, 4:5], gw[h][:, tt, :], ALU.mult, ALU.mult, [pNk, "sm4", f"gw{h}"], ["hs"])
                pt, ptk = ptr[0]
                p.op("pe", lambda e: [e.transpose(pt[:, fc * 128:(fc + 1) * 128], hs[:, fc * 128:(fc + 1) * 128], ident[:])
                                      for fc in range(4)][-1], ["hs", "ident"], [ptk])
                o_cp(p, "act", hsT[h][:, :, tsl], pt[:, 0:512].rearrange("p (f t) -> p f t", t=128), [ptk], [f"hsT{h}"])
        for h in range(4):
            p.dma("sp", lambda e, h=h, t0=t0: e.dma_start(
                out=yT_d[h * 512:(h + 1) * 512, t0:t0 + TB].rearrange("(f q) t -> q f t", q=128), in_=hsT[h]),
                r=[f"hsT{h}"], sem="yout")
        p.barrier()


def decl_mixC(nc, S, D, sfx="", with_x=True, out_kind="ExternalOutput"):
    KC = D // 128
    dt = nc.dram_tensor
    io = {
        "g_mix": dt("g_mix" + sfx, [128, KC], F32, kind="ExternalInput").ap(),
        "w_in": dt("w_in" + sfx, [50, 128, KC, 128], F32, kind="ExternalInput").ap(),
        "gate_b": dt("gate_b", [4, 2], F32, kind="ExternalInput").ap(),
        "norm_w": dt("norm_w", [1, 2048], F32, kind="ExternalInput").ap(),
    }
    if with_x:
        io["x"] = dt("x" + sfx, [S, D], F32, kind="ExternalInput").ap()
    if out_kind is not None:
        io["yT"] = dt("yT", [2048, S], BF16, kind=out_kind).ap()
    return io


def build_mixC(S=SEQ, D=D_MODEL):
    nc = bass.Bass("TRN2", target_bir_lowering=False)
    io = decl_mixC(nc, S, D)
    with ExitStack() as stack:
        p = Prog(nc, stack)
        body_mixC(p, nc, io, S, D)
        p.op("sp", None, w=[f"hsT{h}" for h in range(4)])
        p.emit()
    return nc


PAIRS = [[0, 1], [2, 3], [4, 5], [6, 7]]
_PARITY = {}


def _parity(e):
    k = id(e)
    if k not in _PARITY:
        j = e.snap(e.partition_id() % 2, min_val=0, max_val=1)
        jn = e.snap(1 - j, min_val=0, max_val=1)
        _PARITY[k] = (j, jn)
    return _PARITY[k]


def _emit_pair_gather(p, name, n, r, Gin, Gout, zv, src_fn):
    keys = []
    for i in range(n):
        if src_fn is not None:
            def f_my(e, i=i):
                j, jn = _parity(e)
                return e.dma_start(out=Gin[i].rearrange("(s r) c -> s r c", s=2)[bass.ds(j, 1)][0], in_=src_fn(i))
            p.dma("sp", f_my, w=[f"{name}_in{i}a"], sem=f"{name}_w")
            keys.append(f"{name}_in{i}a")

        def f_z(e, i=i):
            j, jn = _parity(e)
            return e.dma_start(out=Gin[i].rearrange("(s r) c -> s r c", s=2)[bass.ds(jn, 1)][0], in_=zv)
        p.dma("sp", f_z, w=[f"{name}_in{i}b"], sem=f"{name}_w")
        keys.append(f"{name}_in{i}b")
    return keys


def _emit_pair_cc(p, name, n, Gin, Gout, keys):
    for i in range(n):
        p.dma("pool", lambda e, i=i: e.collective_compute("AllReduce", ALU.add, replica_groups=PAIRS,
                                                           ins=[Gin[i].opt()], outs=[Gout[i].opt()]),
              r=keys, w=[f"{name}_out{i}"], sem=f"{name}_cc", inc=1)


def build_fused(S=SEQ, D=D_MODEL, F=D_FF):
    nc = bass.Bass("TRN2", target_bir_lowering=False)
    _PARITY.clear()
    H2 = S // 2
    KC = D // 128
    dt = nc.dram_tensor
    ioA = decl_mixA(nc, S, D, sfx="A", out_kind=None)
    ioC = decl_mixC(nc, S, D, sfx="C", with_x=False, out_kind=None)
    xh_d = dt("x_half", [H2, D], F32, kind="ExternalInput").ap()
    gm1_d = dt("g_mix1", [128, KC], F32, kind="ExternalInput").ap()
    zer_d = dt("zeros", [256, 2048], BF16, kind="ExternalInput").ap()
    gfin_d = dt("g_fin", [1, D], F32, kind="ExternalInput").ap()
    lw = []
    for l in range(2):
        lw.append({
            "w_out": dt(f"w_out{l}", [D // 256, 128, KC, 256], F32, kind="ExternalInput").ap(),
            "w_up": dt(f"w_up{l}", [F // 256, 128, KC, 256], F32, kind="ExternalInput").ap(),
            "w_down": dt(f"w_down{l}", [F, D], F32, kind="ExternalInput").ap(),
            "g_mlp": dt(f"g_mlp{l}", [128, KC], F32, kind="ExternalInput").ap()})
    out_d = dt("out", [H2, D], F32, kind="ExternalOutput").ap()
    yA = dt("yA", [2048, S], BF16).ap()
    yC = dt("yC", [2048, S], BF16).ap()
    x1s = dt("x1s", [H2, D], F32).ap()
    G1in = [dt(f"g1in{i}", [512, S], BF16).ap() for i in range(8)]
    G1out = [dt(f"g1out{i}", [512, S], BF16).ap() for i in range(8)]
    G2in = [dt(f"g2in{i}", [1024, H2], BF16).ap() for i in range(8)]
    G2out = [dt(f"g2out{i}", [1024, H2], BF16).ap() for i in range(8)]
    G3in = [dt(f"g3in{i}", [512, S], BF16).ap() for i in range(8)]
    G3out = [dt(f"g3out{i}", [512, S], BF16).ap() for i in range(8)]
    z1 = zer_d
    z2 = zer_d.rearrange("r (h c) -> (r h) c", h=2)

    with ExitStack() as stack:
        p = Prog(nc, stack)
        master = Arena(p, "master", 204 * 1024)
        banks = [p.ps(f"bank{i}", [128, 512]) for i in range(8)]
        p.master = master
        p.banks = banks

        def new_phase():
            p.barrier()
            master.reset()
            p.bank_i = 0

        ioA["yT"] = yA
        body_mixA(p, nc, ioA, S, D)
        p.barrier()
        k1 = _emit_pair_gather(p, "g1", 8, 256, G1in, G1out, z1, lambda i: yA[i * 256:(i + 1) * 256, :])
        _emit_pair_cc(p, "g1", 8, G1in, G1out, k1)

        new_phase()
        stB = {}

        def mk_load(Gout, gname, mapping):
            def load_hT(ps_, hT, keys):
                for (kc0, piece, owner) in mapping:
                    def f(e, kc0=kc0, piece=piece, owner=owner, ps_=ps_):
                        j, jn = _parity(e)
                        src = Gout[piece].rearrange("r (h x) -> h r x", h=2)[bass.ds(j, 1)][
                            0, owner * 256:(owner + 1) * 256, ps_ * 512:(ps_ + 1) * 512]
                        return e.dma_start(out=hT[:, kc0:kc0 + 2, :], in_=src.rearrange("(c q) t -> q c t", q=128))
                    p.dma("sp", f, r=[f"{gname}_out{piece}"], w=keys, sem="hT")
            return load_hT

        mapB = [(t * 16 + o * 8 + pg_ * 2, t * 4 + pg_, o) for t in range(2) for o in range(2) for pg_ in range(4)]
        mapD = [(o * 16 + pc * 2, pc, o) for o in range(2) for pc in range(8)]

        def after_pass(ps_, acc, hT, xn, junk, ssum, ms, rstd, ident, ps_tr, TT, DB):
            if "gT1" not in stB:
                stB["gT1"] = p.sb("gT1", [128, KC], F32)
                p.dma("sp", lambda e: e.dma_start(out=stB["gT1"][:], in_=gm1_d), w=["gT1"], sem="gT1")
            gT1 = stB["gT1"]
            for tt in range(TT):
                _rms_tile(p, acc, tt, DB, hT, gT1, xn, junk, ssum, ms, rstd, ident, ps_tr, D, gkey="gT1")
            hk = [f"hT{tt}" for tt in range(TT)]
            for i in range(8):
                def f_my(e, i=i, ps_=ps_):
                    j, jn = _parity(e)
                    dst = G2in[i].rearrange("(s r) t -> s r t", s=2)[bass.ds(j, 1)][0, :, ps_ * 512:(ps_ + 1) * 512]
                    return e.dma_start(out=dst.rearrange("(c q) t -> q c t", q=128), in_=hT[:, 4 * i:4 * i + 4, :])
                p.dma("sp", f_my, r=hk, w=[f"g2_in{i}a{ps_}"], sem="g2_w")

                def f_z(e, i=i, ps_=ps_):
                    j, jn = _parity(e)
                    return e.dma_start(out=G2in[i].rearrange("(s r) t -> s r t", s=2)[bass.ds(jn, 1)][
                        0, :, ps_ * 512:(ps_ + 1) * 512], in_=z2[:, 0:512])
                p.dma("sp", f_z, w=[f"g2_in{i}b{ps_}"], sem="g2_w")
                stB.setdefault("keys", []).extend([f"g2_in{i}a{ps_}", f"g2_in{i}b{ps_}"])

        ioB = dict(lw[0])
        ioB.update({"x": xh_d, "out": x1s})
        body_post(p, ioB, H2, D, F, False, fz={"load_hT": mk_load(G1out, "g1", mapB), "after_pass": after_pass})
        p.barrier()
        _emit_pair_cc(p, "g2", 8, G2in, G2out, stB["keys"])

        new_phase()

        def load_hT_C(b, hT, keys):
            slot, half = b // 2, b % 2
            for i in range(8):
                p.dma("sp", lambda e, i=i: e.dma_start(
                    out=hT[:, 4 * i:4 * i + 4, :],
                    in_=G2out[i][slot * 512:(slot + 1) * 512, half * 512:(half + 1) * 512].rearrange("(c q) t -> q c t", q=128)),
                    r=[f"g2_out{i}"], w=keys, sem="hTl")
        ioC["yT"] = yC
        body_mixC(p, nc, ioC, S, D, fz={"load_hT": load_hT_C})
        p.barrier()
        k3 = _emit_pair_gather(p, "g3", 8, 256, G3in, G3out, z1, lambda i: yC[i * 256:(i + 1) * 256, :])
        _emit_pair_cc(p, "g3", 8, G3in, G3out, k3)

        new_phase()
        ioD = dict(lw[1])
        ioD.update({"x": x1s, "out": out_d, "g_fin": gfin_d})
        TT, DB = body_post(p, ioD, H2, D, F, True, fz={"load_hT": mk_load(G3out, "g3", mapD)})
        p.op("sp", None, w=[f"acc{tt}_{db}" for tt in range(TT) for db in range(DB)])
        p.emit()
    return nc


def _til(v, n):
    return np.ascontiguousarray(np.asarray(v, np.float32).reshape(n, 128).T)


def _tile_w(w, cols_per_tile, width):
    Dm = w.shape[0]
    T = len(cols_per_tile)
    sel = np.zeros((Dm, T, width), np.float32)
    for t, cols in enumerate(cols_per_tile):
        sel[:, t, :len(cols)] = w[:, cols]
    return np.ascontiguousarray(sel.reshape(Dm // 128, 128, T, width).transpose(2, 1, 0, 3))


def _run(nc, maps):
    res = run_bass_kernel_spmd(nc, maps, core_ids=list(range(len(maps))))
    return res.results


def kernel_unfused(x, norm_mix, norm_mlp, norm_final, mlp_up, mlp_down, hy_in, lru_conv_w, lru_conv_b,
           lru_wa, lru_ba, lru_wx, lru_bx, lru_lam, rwkv_mu, rwkv_w0, rwkv_w2, rwkv_a0, rwkv_a2,
           rwkv_g2, rwkv_kk, rwkv_ka, rwkv_rk, rwkv_ln_w, rwkv_ln_b, hy_out, ml_in, ml_bi, ml_bf,
           ml_norm, ml_out):
    f32 = np.float32
    A = lambda a: np.asarray(a, dtype=f32)
    x = A(x)
    B, S, D = x.shape
    KC = D // 128
    H2 = S // 2
    hy = A(hy_in)[0]
    mapsA_j = []
    for j in range(2):
        tiles = []
        for h in range(4):
            hh = 4 * j + h
            for base in (0, 2048):
                for c2 in range(2):
                    tiles.append(list(range(base + hh * 256 + c2 * 128, base + hh * 256 + (c2 + 1) * 128)))
        lo = 4096 + 6144
        tiles += [list(range(lo, lo + 96)), list(range(lo + 96, lo + 192)), list(range(lo + 192, lo + 320)), list(range(lo + 320, lo + 448))]
        for q in range(8):
            c0 = j * 1024 + q * 128
            for part in range(3):
                tiles.append(list(range(4096 + part * 2048 + c0, 4096 + part * 2048 + c0 + 128)))
        w_in = _tile_w(hy, tiles, 128)
        ch = slice(j * 1024, (j + 1) * 1024)
        cw = A(lru_conv_w)[0][:, ch]
        lp = np.stack([_til(cw[0], 8), _til(cw[1], 8), _til(cw[2], 8), _til(cw[3], 8), _til(A(lru_conv_b)[0][ch], 8),
                       _til(A(lru_ba)[0][ch], 8), _til(A(lru_bx)[0][ch], 8), _til(A(lru_lam)[0][ch], 8)], axis=2)
        mu = A(rwkv_mu)[0]
        rp = np.stack([_til(mu[0:2048][ch], 8), _til(mu[2048:4096][ch], 8), _til(mu[4096:6144][ch], 8),
                       _til(A(rwkv_w0)[0][ch], 8), _til(A(rwkv_a0)[0][ch], 8), _til(A(rwkv_kk)[0][ch], 8),
                       _til(A(rwkv_ka)[0][ch], 8), _til(A(rwkv_rk)[0][ch], 8)], axis=2)
        rl = np.zeros((128, 4), f32)
        rl[:96, 0] = mu[6144:6240]
        rl[:96, 1] = mu[6240:6336]
        rl[:, 2] = mu[6336:6464]
        rl[:, 3] = mu[6464:6592]
        lnw = A(rwkv_ln_w)[0][ch]
        lnb = A(rwkv_ln_b)[0][ch]
        ln = np.zeros((128, 2, 8, 64), f32)
        for q in range(8):
            for h in range(2):
                ln[h * 64:(h + 1) * 64, 0, q, :] = lnw[(2 * q + h) * 64:(2 * q + h + 1) * 64][None, :]
                ln[h * 64:(h + 1) * 64, 1, q, :] = lnb[(2 * q + h) * 64:(2 * q + h + 1) * 64][None, :]
        mapsA_j.append({
            "g_mix": _til(A(norm_mix)[0], KC), "w_in": w_in, "lru_p": np.ascontiguousarray(lp),
            "lru_wa": np.ascontiguousarray(A(lru_wa)[0][4 * j:4 * j + 4]), "lru_wx": np.ascontiguousarray(A(lru_wx)[0][4 * j:4 * j + 4]),
            "rw_p": np.ascontiguousarray(rp), "rw_lmu": rl, "rw_w2": np.ascontiguousarray(A(rwkv_w2)[0][:, ch]),
            "rw_a2": np.ascontiguousarray(A(rwkv_a2)[0][:, ch]), "rw_g2": np.ascontiguousarray(A(rwkv_g2)[0][:, ch]), "rw_ln": ln})
    mapsA = []
    for c in range(8):
        b, j = c // 2, c % 2
        m = dict(mapsA_j[j])
        m["x"] = np.ascontiguousarray(x[b])
        mapsA.append(m)
    resA = _run(build_mixA(S=S, D=D), mapsA)

    def _ctile(w, width):
        Dm, Nn = w.shape
        return np.ascontiguousarray(w.reshape(Dm // 128, 128, Nn // width, width).transpose(2, 1, 0, 3))

    def post(res_mix, xcur, layer, w_out, final):
        maps = []
        w_out_t = _ctile(w_out, 256)
        w_up_t = _ctile(A(mlp_up)[layer], 256)
        for c in range(8):
            b, j = c // 2, c % 2
            ts = slice(j * H2, (j + 1) * H2)
            y0, y1 = res_mix[2 * b]["yT"], res_mix[2 * b + 1]["yT"]
            if layer == 0:
                yT = np.concatenate([y0[0:1024, ts], y1[0:1024, ts], y0[1024:2048, ts], y1[1024:2048, ts]], axis=0)
            else:
                yT = np.concatenate([y0[:, ts], y1[:, ts]], axis=0)
            m = {"x": np.ascontiguousarray(xcur[b, ts]), "yT": np.ascontiguousarray(yT), "w_out": w_out_t,
                 "w_up": w_up_t, "w_down": A(mlp_down)[layer], "g_mlp": _til(A(norm_mlp)[layer], KC)}
            if final:
                m["g_fin"] = A(norm_final).reshape(1, D)
            maps.append(m)
        res = _run(build_post(NT=H2, D=D, F=A(mlp_up).shape[2], final=final), maps)
        out = np.empty_like(xcur)
        for c in range(8):
            b, j = c // 2, c % 2
            out[b, j * H2:(j + 1) * H2] = res[c]["out"]
        return out

    x1 = post(resA, x, 0, A(hy_out)[0], False)
    mi = A(ml_in)[0]
    mapsC_j = []
    for j in range(2):
        tiles = []
        for h in range(4):
            hh = 4 * j + h
            for base in (0, 2048):
                for dc in range(2):
                    tiles.append(list(range(base + hh * 256 + dc * 128, base + hh * 256 + (dc + 1) * 128)))
        for base in (4096, 8192):
            for h in range(4):
                hh = 4 * j + h
                for cc in range(4):
                    tiles.append(list(range(base + hh * 512 + cc * 128, base + hh * 512 + (cc + 1) * 128)))
        tiles.append(list(range(12288 + 4 * j, 12288 + 4 * j + 4)))
        tiles.append(list(range(12296 + 4 * j, 12296 + 4 * j + 4)))
        w_inC = _tile_w(mi, tiles, 128)
        mapsC_j.append({
            "g_mix": _til(A(norm_mix)[1], KC), "w_in": w_inC,
            "gate_b": np.ascontiguousarray(np.stack([A(ml_bi)[0][4 * j:4 * j + 4], A(ml_bf)[0][4 * j:4 * j + 4]], axis=1)),
            "norm_w": np.ascontiguousarray(A(ml_norm)[0][j * 2048:(j + 1) * 2048].reshape(1, 2048))})
    mapsC = []
    for c in range(8):
        b, j = c // 2, c % 2
        m = dict(mapsC_j[j])
        m["x"] = np.ascontiguousarray(x1[b])
        mapsC.append(m)
    resC = _run(build_mixC(S=S, D=D), mapsC)
    out = post(resC, x1, 1, A(ml_out)[0], True)
    return out


def _mixA_host(j, hy, norm_mix, lru_conv_w, lru_conv_b, lru_wa, lru_ba, lru_wx, lru_bx, lru_lam, rwkv_mu, rwkv_w0,
               rwkv_w2, rwkv_a0, rwkv_a2, rwkv_g2, rwkv_kk, rwkv_ka, rwkv_rk, rwkv_ln_w, rwkv_ln_b, KC):
    f32 = np.float32
    A = lambda a: np.asarray(a, dtype=f32)
    tiles = []
    for h in range(4):
        hh = 4 * j + h
        for base in (0, 2048):
            for c2 in range(2):
                tiles.append(list(range(base + hh * 256 + c2 * 128, base + hh * 256 + (c2 + 1) * 128)))
    lo = 4096 + 6144
    tiles += [list(range(lo, lo + 96)), list(range(lo + 96, lo + 192)), list(range(lo + 192, lo + 320)), list(range(lo + 320, lo + 448))]
    for q in range(8):
        c0 = j * 1024 + q * 128
        for part in range(3):
            tiles.append(list(range(4096 + part * 2048 + c0, 4096 + part * 2048 + c0 + 128)))
    w_in = _tile_w(hy, tiles, 128)
    ch = slice(j * 1024, (j + 1) * 1024)
    cw = A(lru_conv_w)[0][:, ch]
    lp = np.stack([_til(cw[0], 8), _til(cw[1], 8), _til(cw[2], 8), _til(cw[3], 8), _til(A(lru_conv_b)[0][ch], 8),
                   _til(A(lru_ba)[0][ch], 8), _til(A(lru_bx)[0][ch], 8), _til(A(lru_lam)[0][ch], 8)], axis=2)
    mu = A(rwkv_mu)[0]
    rp = np.stack([_til(mu[0:2048][ch], 8), _til(mu[2048:4096][ch], 8), _til(mu[4096:6144][ch], 8),
                   _til(A(rwkv_w0)[0][ch], 8), _til(A(rwkv_a0)[0][ch], 8), _til(A(rwkv_kk)[0][ch], 8),
                   _til(A(rwkv_ka)[0][ch], 8), _til(A(rwkv_rk)[0][ch], 8)], axis=2)
    rl = np.zeros((128, 4), f32)
    rl[:96, 0] = mu[6144:6240]
    rl[:96, 1] = mu[6240:6336]
    rl[:, 2] = mu[6336:6464]
    rl[:, 3] = mu[6464:6592]
    lnw = A(rwkv_ln_w)[0][ch]
    lnb = A(rwkv_ln_b)[0][ch]
    ln = np.zeros((128, 2, 8, 64), f32)
    for q in range(8):
        for h in range(2):
            ln[h * 64:(h + 1) * 64, 0, q, :] = lnw[(2 * q + h) * 64:(2 * q + h + 1) * 64][None, :]
            ln[h * 64:(h + 1) * 64, 1, q, :] = lnb[(2 * q + h) * 64:(2 * q + h + 1) * 64][None, :]
    return {"w_in": w_in, "lru_p": np.ascontiguousarray(lp),
            "lru_wa": np.ascontiguousarray(A(lru_wa)[0][4 * j:4 * j + 4]), "lru_wx": np.ascontiguousarray(A(lru_wx)[0][4 * j:4 * j + 4]),
            "rw_p": np.ascontiguousarray(rp), "rw_lmu": rl, "rw_w2": np.ascontiguousarray(A(rwkv_w2)[0][:, ch]),
            "rw_a2": np.ascontiguousarray(A(rwkv_a2)[0][:, ch]), "rw_g2": np.ascontiguousarray(A(rwkv_g2)[0][:, ch]), "rw_ln": ln}


def _mixC_host(j, mi, ml_bi, ml_bf, ml_norm):
    f32 = np.float32
    A = lambda a: np.asarray(a, dtype=f32)
    tiles = []
    for h in range(4):
        hh = 4 * j + h
        for base in (0, 2048):
            for dc in range(2):
                tiles.append(list(range(base + hh * 256 + dc * 128, base + hh * 256 + (dc + 1) * 128)))
    for base in (4096, 8192):
        for h in range(4):
            hh = 4 * j + h
            for cc in range(4):
                tiles.append(list(range(base + hh * 512 + cc * 128, base + hh * 512 + (cc + 1) * 128)))
    tiles.append(list(range(12288 + 4 * j, 12288 + 4 * j + 4)))
    tiles.append(list(range(12296 + 4 * j, 12296 + 4 * j + 4)))
    return {"w_in": _tile_w(mi, tiles, 128),
            "gate_b": np.ascontiguousarray(np.stack([A(ml_bi)[0][4 * j:4 * j + 4], A(ml_bf)[0][4 * j:4 * j + 4]], axis=1)),
            "norm_w": np.ascontiguousarray(A(ml_norm)[0][j * 2048:(j + 1) * 2048].reshape(1, 2048))}


def kernel(x, norm_mix, norm_mlp, norm_final, mlp_up, mlp_down, hy_in, lru_conv_w, lru_conv_b,
           lru_wa, lru_ba, lru_wx, lru_bx, lru_lam, rwkv_mu, rwkv_w0, rwkv_w2, rwkv_a0, rwkv_a2,
           rwkv_g2, rwkv_kk, rwkv_ka, rwkv_rk, rwkv_ln_w, rwkv_ln_b, hy_out, ml_in, ml_bi, ml_bf,
           ml_norm, ml_out):
    f32 = np.float32
    A = lambda a: np.asarray(a, dtype=f32)
    x = A(x)
    B, S, D = x.shape
    KC = D // 128
    H2 = S // 2
    F = A(mlp_up).shape[2]

    def ctile(w, width):
        Dm, Nn = w.shape
        return np.ascontiguousarray(w.reshape(Dm // 128, 128, Nn // width, width).transpose(2, 1, 0, 3))

    hy = A(hy_in)[0]
    mi = A(ml_in)[0]
    hostA = [_mixA_host(j, hy, norm_mix, lru_conv_w, lru_conv_b, lru_wa, lru_ba, lru_wx, lru_bx, lru_lam, rwkv_mu, rwkv_w0,
                        rwkv_w2, rwkv_a0, rwkv_a2, rwkv_g2, rwkv_kk, rwkv_ka, rwkv_rk, rwkv_ln_w, rwkv_ln_b, KC) for j in range(2)]
    hostC = [_mixC_host(j, mi, ml_bi, ml_bf, ml_norm) for j in range(2)]
    shared = {
        "g_mixA": _til(A(norm_mix)[0], KC), "g_mixC": _til(A(norm_mix)[1], KC), "g_mix1": _til(A(norm_mix)[1], KC),
        "zeros": np.zeros((256, 2048), ml_dtypes.bfloat16), "g_fin": A(norm_final).reshape(1, D),
        "w_out0": ctile(A(hy_out)[0], 256), "w_up0": ctile(A(mlp_up)[0], 256), "w_down0": np.ascontiguousarray(A(mlp_down)[0]),
        "g_mlp0": _til(A(norm_mlp)[0], KC),
        "w_out1": ctile(A(ml_out)[0], 256), "w_up1": ctile(A(mlp_up)[1], 256), "w_down1": np.ascontiguousarray(A(mlp_down)[1]),
        "g_mlp1": _til(A(norm_mlp)[1], KC),
    }
    maps = []
    for c in range(8):
        b, j = c // 2, c % 2
        m = dict(shared)
        for k, v in hostA[j].items():
            m["w_inA" if k == "w_in" else k] = v
        for k, v in hostC[j].items():
            m["w_inC" if k == "w_in" else k] = v
        m["xA"] = np.ascontiguousarray(x[b])
        m["x_half"] = np.ascontiguousarray(x[b, j * H2:(j + 1) * H2])
        maps.append(m)
    res = _run(build_fused(S=S, D=D, F=F), maps)
    out = np.empty_like(x)
    for c in range(8):
        b, j = c // 2, c % 2
        out[b, j * H2:(j + 1) * H2] = res[c]["out"]
    return out
```
